# Optimizing a Trainium2 kernel written in Bass

```python
import math
import jax, jax.numpy as jnp
from jax import lax
import numpy as np

D_MODEL = 1024
BATCH = 8
SEQ = 2048
DEPTH = 1
DEC_BATCH = 128
DEC_SEQ = 1
PAST_LEN = 16384
PAGE_SIZE = 128

N_GLA_HEADS = 4
GLA_DK_HEAD = D_MODEL // (2 * N_GLA_HEADS)
GLA_DV_HEAD = D_MODEL // N_GLA_HEADS
GLA_QK_WIDTH = N_GLA_HEADS * GLA_DK_HEAD
GLA_V_WIDTH = N_GLA_HEADS * GLA_DV_HEAD
GATE_LOW_RANK = 16
GLA_GATE_TAU = 16.0
GLA_CHUNK = 64
D_CONV = D_MODEL
CONV_WIDTH = 3
D_FF = 2816
MACARON_WEIGHT = 0.5
RMS_EPS = 1e-6
N_SUBLAYERS = 3

_O_K = GLA_QK_WIDTH
_O_V = _O_K + GLA_QK_WIDTH
_O_R = _O_V + GLA_V_WIDTH
_O_Z = _O_R + GLA_V_WIDTH
_O_B = _O_Z + GATE_LOW_RANK
_O_C = _O_B + D_CONV
_O_H = _O_C + D_CONV
_O_GA = _O_H + D_CONV
_O_GB = _O_GA + D_MODEL
MIX_IN_WIDTH = _O_GB + D_MODEL
SPLIT_POINTS = (_O_K, _O_V, _O_R, _O_Z, _O_B, _O_C, _O_H, _O_GA, _O_GB)

kernel_name = 'gla_shortconv_macaron_adaln_decoder_step'


def _rmsnorm(x, g):
    x32 = x.astype(jnp.float32)
    y = x32 * lax.rsqrt(jnp.mean(x32 * x32, axis=-1, keepdims=True) + RMS_EPS)
    return (y * g.astype(jnp.float32)).astype(x.dtype)


def _swiglu(h, w_in, w_out):
    gate, up = jnp.split(h @ w_in, 2, axis=-1)
    return (jax.nn.silu(gate) * up) @ w_out


def _gla(q, k, v, log_a, s0):
    n, l = q.shape[0], q.shape[1]
    c = math.gcd(l, GLA_CHUNK)
    nc = l // c

    def to_chunks(t):
        return t.reshape(n, nc, c, N_GLA_HEADS, t.shape[-1]).transpose(1, 0, 3, 2, 4).astype(jnp.float32)

    causal = jnp.tril(jnp.ones((c, c), dtype=bool))[:, :, None]

    def step(s, inp):
        qc, kc, vc, ac = inp
        b = jnp.cumsum(ac, axis=2)
        diff = jnp.where(causal, b[:, :, :, None, :] - b[:, :, None, :, :], -jnp.inf)
        scores = jnp.einsum('nhtk,nhsk,nhtsk->nhts', qc, kc, jnp.exp(diff))
        o = jnp.einsum('nhts,nhsv->nhtv', scores, vc) + jnp.einsum('nhtk,nhkv->nhtv', qc * jnp.exp(b), s)
        b_last = b[:, :, -1:, :]
        s_new = jnp.exp(b_last[:, :, 0, :])[..., None] * s + jnp.einsum('nhsk,nhsv->nhkv', kc * jnp.exp(b_last - b), vc)
        return s_new, o

    s_fin, o = lax.scan(step, s0.astype(jnp.float32), (to_chunks(q), to_chunks(k), to_chunks(v), to_chunks(log_a)))
    o = o.transpose(1, 0, 3, 2, 4).reshape(n, l, N_GLA_HEADS, GLA_DV_HEAD)
    return o, s_fin.astype(s0.dtype)


def _token_mixer(h, s_gla, s_conv, w_mix_in, w_alpha, b_alpha, g_gla_norm, w_conv, w_branch_out, w_mix_out):
    n, l, _ = h.shape
    proj = h @ w_mix_in
    q, k, v, r, z, cb, cc, ch, ga, gb = jnp.split(proj, SPLIT_POINTS, axis=-1)

    def heads(t):
        return t.reshape(n, l, N_GLA_HEADS, -1)

    log_a = jax.nn.log_sigmoid((z @ w_alpha + b_alpha).astype(jnp.float32)) / GLA_GATE_TAU
    o, s_gla_new = _gla(heads(q) * (GLA_DK_HEAD ** -0.5), heads(k), heads(v), heads(log_a), s_gla)
    o = _rmsnorm(o.astype(h.dtype), g_gla_norm.reshape(N_GLA_HEADS, GLA_DV_HEAD)).reshape(n, l, GLA_V_WIDTH)
    y_gla = o * jax.nn.silu(r)

    u = cc * ch
    padded = jnp.concatenate([s_conv.astype(u.dtype), u], axis=1)
    conv = (w_conv[0] * padded[:, 0:l] + w_conv[1] * padded[:, 1:l + 1] + w_conv[2] * padded[:, 2:l + 2])
    s_conv_new = padded[:, l:]
    y_conv = cb * conv

    branches = jnp.stack([y_gla, y_conv], axis=-2)
    proj_b = jnp.einsum('nlbw,bwd->nlbd', branches, w_branch_out)
    gates = jax.nn.sigmoid(jnp.stack([ga, gb], axis=-2))
    merged = jnp.sum(gates * proj_b, axis=-2)
    return merged @ w_mix_out, s_gla_new, s_conv_new


def _decoder_layer(x, c, s_gla, s_conv, w_ada, b_ada, g_pre, g_post, w_ffn1_in, w_ffn1_out,
                   w_ffn2_in, w_ffn2_out, w_mix_in, w_alpha, b_alpha, g_gla_norm, w_conv,
                   w_branch_out, w_mix_out):
    ada = (jax.nn.silu(c) @ w_ada + b_ada).reshape(c.shape[0], N_SUBLAYERS, 3, 1, D_MODEL)
    shift, scale, gate = ada[:, :, 0], ada[:, :, 1], ada[:, :, 2]

    def pre(i, t):
        return _rmsnorm(t, g_pre[i]) * (1.0 + scale[:, i]) + shift[:, i]

    def post(i, t, out, weight):
        return t + weight * gate[:, i] * _rmsnorm(out, g_post[i])

    x = post(0, x, _swiglu(pre(0, x), w_ffn1_in, w_ffn1_out), MACARON_WEIGHT)
    mix, s_gla_new, s_conv_new = _token_mixer(pre(1, x), s_gla, s_conv, w_mix_in, w_alpha, b_alpha,
                                              g_gla_norm, w_conv, w_branch_out, w_mix_out)
    x = post(1, x, mix, 1.0)
    x = post(2, x, _swiglu(pre(2, x), w_ffn2_in, w_ffn2_out), MACARON_WEIGHT)
    return x, s_gla_new, s_conv_new


def setup_inputs(seed: int = 0) -> dict:
    key = jax.random.key(seed)
    ks = jax.random.split(key, 24)

    def nrm(k, shape, scale):
        return jax.random.normal(k, shape, jnp.float32) * scale

    return {
        'x_prompt': nrm(ks[0], (BATCH, SEQ, D_MODEL), 1.0),
        'x_sample': nrm(ks[1], (DEC_BATCH, DEC_SEQ, D_MODEL), 1.0),
        'state_gla': nrm(ks[2], (DEPTH, DEC_BATCH, N_GLA_HEADS, GLA_DK_HEAD, GLA_DV_HEAD), 1.0),
        'state_conv': nrm(ks[3], (DEPTH, DEC_BATCH, CONV_WIDTH - 1, D_CONV), 1.0),
        'c_prompt': nrm(ks[4], (BATCH, D_MODEL), 1.0),
        'c_sample': nrm(ks[5], (DEC_BATCH, D_MODEL), 1.0),
        'w_ada': nrm(ks[6], (DEPTH, D_MODEL, N_SUBLAYERS * 3 * D_MODEL), D_MODEL ** -0.5),
        'b_ada': nrm(ks[7], (DEPTH, N_SUBLAYERS * 3 * D_MODEL), 0.02),
        'g_pre': 1.0 + nrm(ks[8], (DEPTH, N_SUBLAYERS, D_MODEL), 0.02),
        'g_post': 1.0 + nrm(ks[9], (DEPTH, N_SUBLAYERS, D_MODEL), 0.02),
        'w_ffn1_in': nrm(ks[10], (DEPTH, D_MODEL, 2 * D_FF), D_MODEL ** -0.5),
        'w_ffn1_out': nrm(ks[11], (DEPTH, D_FF, D_MODEL), D_FF ** -0.5),
        'w_ffn2_in': nrm(ks[12], (DEPTH, D_MODEL, 2 * D_FF), D_MODEL ** -0.5),
        'w_ffn2_out': nrm(ks[13], (DEPTH, D_FF, D_MODEL), D_FF ** -0.5),
        'w_mix_in': nrm(ks[14], (DEPTH, D_MODEL, MIX_IN_WIDTH), D_MODEL ** -0.5),
        'w_alpha': nrm(ks[15], (DEPTH, GATE_LOW_RANK, GLA_QK_WIDTH), GATE_LOW_RANK ** -0.5),
        'b_alpha': nrm(ks[16], (DEPTH, GLA_QK_WIDTH), 0.02),
        'g_gla_norm': 1.0 + nrm(ks[17], (DEPTH, GLA_V_WIDTH), 0.02),
        'w_conv': nrm(ks[18], (DEPTH, CONV_WIDTH, D_CONV), CONV_WIDTH ** -0.5),
        'w_branch_out': nrm(ks[19], (DEPTH, 2, D_MODEL, D_MODEL), D_MODEL ** -0.5),
        'w_mix_out': nrm(ks[20], (DEPTH, D_MODEL, D_MODEL), D_MODEL ** -0.5),
    }


def reference(x_prompt, x_sample, state_gla, state_conv, c_prompt, c_sample, w_ada, b_ada, g_pre,
              g_post, w_ffn1_in, w_ffn1_out, w_ffn2_in, w_ffn2_out, w_mix_in, w_alpha, b_alpha,
              g_gla_norm, w_conv, w_branch_out, w_mix_out):
    yp, ys = x_prompt, x_sample
    n_p = x_prompt.shape[0]
    gla_p, conv_p, gla_s, conv_s = [], [], [], []
    for i in range(DEPTH):
        wl = (w_ada[i], b_ada[i], g_pre[i], g_post[i], w_ffn1_in[i], w_ffn1_out[i], w_ffn2_in[i],
              w_ffn2_out[i], w_mix_in[i], w_alpha[i], b_alpha[i], g_gla_norm[i], w_conv[i],
              w_branch_out[i], w_mix_out[i])
        s_gla0 = jnp.zeros((n_p, N_GLA_HEADS, GLA_DK_HEAD, GLA_DV_HEAD), x_prompt.dtype)
        s_conv0 = jnp.zeros((n_p, CONV_WIDTH - 1, D_CONV), x_prompt.dtype)
        yp, sg, sc = _decoder_layer(yp, c_prompt, s_gla0, s_conv0, *wl)
        gla_p.append(sg)
        conv_p.append(sc)
        ys, sg, sc = _decoder_layer(ys, c_sample, state_gla[i], state_conv[i], *wl)
        gla_s.append(sg)
        conv_s.append(sc)
    return (yp, ys, jnp.stack(gla_p), jnp.stack(conv_p), jnp.stack(gla_s), jnp.stack(conv_s))
```

```python
import math
from contextlib import ExitStack

import numpy as np
import concourse.bass as bass
import concourse.mybir as mybir
from concourse.bass_utils import run_bass_kernel_spmd

F32 = mybir.dt.float32
BF16 = mybir.dt.bfloat16
AF = mybir.ActivationFunctionType
ALU = mybir.AluOpType

D = 1024
KC = 8
DFF = 2816
FC = 22
SEQ = 2048
NS = 16
SBT = 1024
NT = SBT + NS
NCOL = SEQ + NS
MIXW = 8208
O_Q, O_K, O_V, O_R, O_Z, O_B, O_C, O_H, O_GA, O_GB = 0, 512, 1024, 2048, 3072, 3088, 4112, 5136, 6160, 7184
EPS = 1e-6
V_BADA, V_GPRE, V_GPOST, V_WCONV = 0, 72, 96, 120
NVEC = 144
NSL = 74
SL_ADA, SL_F1A, SL_F1B, SL_CONV, SL_MERGE, SL_OPROJ, SL_F2A, SL_F2B = 0, 18, 29, 37, 45, 53, 55, 66


class Tok:
    def __init__(self, name, step):
        self.name = name
        self.step = step
        self.count = 0
        self.sem = None


class Eng(Tok):
    def __init__(self, name):
        super().__init__(name, 1)
        self.ops = []
        self.waited = {}


class Res:
    __slots__ = ("name", "last_w", "readers")

    def __init__(self, name):
        self.name = name
        self.last_w = None
        self.readers = []


class Rec:
    def __init__(self):
        self.calls = []

    def __getattr__(self, name):
        def f(*a, **k):
            self.calls.append((name, a, k))
            return self
        return f


def _rec(fn):
    r = Rec()
    fn(r)
    assert len(r.calls) == 1, r.calls
    return r.calls[0]


class Sched:
    def __init__(self):
        self.engs = {n: Eng(n) for n in ("pe", "act", "dve", "pool", "sp")}
        self.toks = list(self.engs.values())
        self.label = ""
        self.pe_labels = []

    def new_dma_tok(self, name):
        t = Tok(name, 16)
        self.toks.append(t)
        return t

    def _wait(self, E, tok, cnt):
        if cnt <= 0:
            return
        if E.waited.get(tok, 0) < cnt:
            E.waited[tok] = cnt
            E.ops.append(("wait", tok, cnt))

    def _deps(self, E, en, reads, writes):
        need = {}
        for r in reads:
            if r.last_w is not None:
                t, c = r.last_w
                if need.get(t, 0) < c:
                    need[t] = c
        for w in writes:
            if w.last_w is not None:
                t, c = w.last_w
                if need.get(t, 0) < c:
                    need[t] = c
            for (t, c) in w.readers:
                if need.get(t, 0) < c:
                    need[t] = c
        for t, c in need.items():
            if t is E and en == "pe":
                continue
            self._wait(E, t, c)

    def _record(self, me, reads, writes):
        for r in reads:
            r.readers.append(me)
            if len(r.readers) > 64:
                best = {}
                for (t, c) in r.readers:
                    if best.get(t, 0) < c:
                        best[t] = c
                r.readers = list(best.items())
        for w in writes:
            w.last_w = me
            w.readers = []

    def op(self, en, fns, reads=(), writes=()):
        E = self.engs[en]
        if not isinstance(fns, (list, tuple)):
            fns = [fns]
        self._deps(E, en, reads, writes)
        if en == "pe":
            self.pe_labels.extend([self.label] * len(fns))
        for f in fns[:-1]:
            E.ops.append(("ins", _rec(f), None))
        E.count += 1
        E.ops.append(("ins", _rec(fns[-1]), E))
        me = (E, E.count)
        self._record(me, reads, writes)
        return me

    def dma(self, en, tok, fns, reads=(), writes=()):
        E = self.engs[en]
        if not isinstance(fns, (list, tuple)):
            fns = [fns]
        self._wait(E, tok, tok.count)
        self._deps(E, en, reads, writes)
        for f in fns:
            tok.count += 1
            E.ops.append(("ins", _rec(f), tok))
        me = (tok, tok.count)
        self._record(me, reads, writes)
        return me

    def barrier(self):
        for E in self.engs.values():
            for t in self.toks:
                if t is E:
                    continue
                self._wait(E, t, t.count)


def build_nc(dbg_names=()):
    nc = bass.Bass("TRN2", target_bir_lowering=False)

    def din(name, shape):
        return nc.dram_tensor(name, list(shape), F32, kind="ExternalInput").ap()

    def dout(name, shape):
        return nc.dram_tensor(name, list(shape), F32, kind="ExternalOutput").ap()

    xT_d = din("xT", [128, KC, NCOL])
    cT_d = din("cT", [128, KC, 17])
    sgla_d = din("sgla", [NS, 4, 128, 256])
    sconv_d = din("sconvT", [128, KC, 2, NS])
    vec_d = din("vecT", [128, NVEC])
    gnb_d = din("gnb", [128, 1024])
    walpha_d = din("walpha", [17, 512])
    ident_d = din("ident", [128, 128])
    maskT_d = din("maskT", [128, 128])
    triI_d = din("triI", [128, 128])
    triR_d = din("triR", [128, 128])
    i16n_d = din("i16n", [16, 16])
    id16_d = din("id16rep", [128, 16, 16])
    wsl_d = din("wsl", [NSL, 128, 4096])
    wga_d = din("wga", [128, KC, 1040])
    wgb_d = din("wgb", [128, KC, 2048])

    yT_d = dout("yT", [128, KC, NCOL])
    ogp_d = dout("ogla_p", [128, 4, 256])
    ocp_d = dout("oconv_p", [128, KC, 2])
    ogs_d = dout("ogla_s", [128, NS * 4, 256])
    ocs_d = dout("oconv_s", [128, KC, 2, NS])
    dbg_d = {}

    S = Sched()
    es = ExitStack()
    with es:
        def sb(name, shape, dt):
            return es.enter_context(nc.sbuf_tensor(name, list(shape), dt))

        for n, E in S.engs.items():
            E.sem = es.enter_context(nc.semaphore("s_" + n))

        def dtok(name):
            t = S.new_dma_tok(name)
            t.sem = es.enter_context(nc.semaphore("d_" + name))
            return t

        X = sb("X", [128, KC, NT], F32)
        XN = sb("XN", [128, KC, NT], BF16)
        BIG = sb("BIG", [128, 24, NT], BF16)
        OUT = sb("OUT", [128, KC, NT], F32)
        WS = sb("WS", [128, 3, 4096], BF16)
        ARX = sb("ARX", [128, 1792], F32)
        ADA = sb("ADA", [128, 72, 17], F32)
        AALL = sb("AALL", [128, 3, KC, 17], F32)
        GALL = sb("GALL", [128, 3, KC, 17], F32)
        VEC = sb("VEC", [128, NVEC], F32)
        GNB = sb("GNB", [128, 1024], F32)
        IDF = sb("IDF", [128, 128], F32)
        IDB = sb("IDB", [128, 128], BF16)
        ONESB = sb("ONESB", [128, 128], BF16)
        MASKT = sb("MASKT", [128, 128], F32)
        TRII = sb("TRII", [128, 128], F32)
        TRIR = sb("TRIR", [128, 128], F32)
        I16N = sb("I16N", [16, 16], F32)
        ID16 = sb("ID16", [128, 16, 16], F32)
        SST = sb("SST", [128, 4, 256], F32)
        SBF = sb("SBF", [128, 4, 256], BF16)
        TAIL = sb("TAIL", [128, KC, 2], F32)
        SC = sb("SC", [128, KC, 2, NS], F32)
        USM = sb("USM", [128, KC, 2, NS], F32)
        WAUG = sb("WAUG", [17, 512], BF16)
        CT = sb("CT", [128, KC, 17], F32)
        SCT = sb("SCT", [128, KC, 17], BF16)
        EPSD = sb("EPSD", [128, 1], F32)
        LNQ = sb("LNQ", [128, 1], F32)
        RSTD = [sb("RSTD%d" % i, [128, 512], F32) for i in range(2)]
        SQ = [sb("SQ%d" % i, [128, 512], BF16) for i in range(2)]
        TMP = [sb("TMP%d" % i, [128, 512], F32) for i in range(3)]
        UBUF = ARX[:, 0:SBT + 2]
        S0J2 = sb("S0J2", [128, 4, 256], F32)
        S0J1 = sb("S0J1", [128, 4, 256], F32)
        PS = es.enter_context(nc.psum_tensor("PS", [128, 4096], F32))

        def psb(b, nb=1):
            return PS[:, b * 512:(b + nb) * 512]

        rdict = {}

        def R(name):
            r = rdict.get(name)
            if r is None:
                r = Res(name)
                rdict[name] = r
            return r

        RX = [[R("X%d_%d" % (k, t)) for t in range(3)] for k in range(KC)]
        RXN = [[R("XN%d_%d" % (k, t)) for t in range(3)] for k in range(KC)]
        RBIG = [[R("BIG%d_%d" % (k, t)) for t in range(3)] for k in range(24)]
        ROUT = [[R("OUT%d_%d" % (k, t)) for t in range(3)] for k in range(KC)]
        RWS = [R("WS%d" % i) for i in range(3)]
        RPS = [R("PS%d" % i) for i in range(8)]
        RRSTD = [R("RSTD%d" % i) for i in range(2)]
        RSQ = [R("SQ%d" % i) for i in range(2)]
        RTMP = [R("TMP%d" % i) for i in range(3)]
        ALL_OUT = [r for row in ROUT for r in row]
        ALL_WGA = [r for k in range(16, 24) for r in RBIG[k]]

        rot = {"ps": 0, "rstd": 0, "sq": 0, "tmp": 0, "ws": 0, "ld": 0}
        pinned = set()

        def pbank(nb=1):
            while True:
                b = rot["ps"] % 8
                if nb == 2 and b % 2 == 1:
                    rot["ps"] += 1
                    continue
                if any(((b + i) % 8) in pinned for i in range(nb)) or b + nb > 8:
                    rot["ps"] += 1
                    continue
                rot["ps"] += nb
                return b, [RPS[b + i] for i in range(nb)]

        def nxt(kind, n):
            i = rot[kind] % n
            rot[kind] += 1
            return i

        T_WS = [dtok("ws%d" % i) for i in range(3)]
        T_LD = [dtok("ld%d" % i) for i in range(6)]
        T_WGA = dtok("wga")
        T_WGB = dtok("wgb")
        T_OUT = [dtok("out%d" % i) for i in range(4)]
        T_S0 = [dtok("s0_%d" % i) for i in range(4)]
        rot["out"] = 0

        def ld_tok():
            return T_LD[nxt("ld", 6)]

        def out_tok():
            return T_OUT[nxt("out", 4)]

        PE = lambda fns, r=(), w=(): S.op("pe", fns, r, w)
        ACT = lambda fns, r=(), w=(): S.op("act", fns, r, w)
        DVE = lambda fns, r=(), w=(): S.op("dve", fns, r, w)
        POOL = lambda fns, r=(), w=(): S.op("pool", fns, r, w)

        def dbg(name, ap, reads):
            if name not in dbg_names:
                return
            shape = list(ap.shape)
            d = nc.dram_tensor("dbg_" + name, shape, ap.dtype, kind="ExternalOutput").ap()
            dbg_d[name] = d
            S.dma("sp", out_tok(), lambda e: e.dma_start(out=d, in_=ap), reads=reads, writes=[R("dbgout_" + name)])

        state = {"c": 0}
        steps = []
        FENCE = "fence"

        def add_step(loads, compute, lbl="step"):
            compute._lbl = lbl
            steps.append((loads, compute))

        def add_work(fn, lbl="work"):
            f = lambda sv, rs: fn()
            f._lbl = lbl
            steps.append(([], f))

        def add_fence(allow=0):
            steps.append((FENCE, allow))

        def run_steps():
            n = len(steps)
            is_f = lambda st: st[0] is FENCE
            load_idx = [i for i in range(n) if not is_f(steps[i]) and steps[i][0]]
            fences = [i for i in range(n) if is_f(steps[i])]
            num_of = {li: m for m, li in enumerate(load_idx)}
            issued = 0
            c = 0
            for i in range(n):
                if is_f(steps[i]):
                    continue
                nf = n
                allow = 0
                for f in fences:
                    if f > i:
                        nf = f
                        allow = steps[f][1]
                        break
                state["c"] = c

                def can_issue():
                    if issued >= len(load_idx) or issued > c + 2:
                        return False
                    li = load_idx[issued]
                    if li < nf:
                        return True
                    beyond = sum(1 for m in range(issued) if load_idx[m] > nf)
                    return beyond < allow
                while can_issue():
                    loads, _ = steps[load_idx[issued]]
                    sl = issued % 3
                    sv = WS[:, sl, :]
                    fns = []
                    for (dst_fn, src) in loads:
                        dst = dst_fn(sv)
                        fns.append(lambda e, dst=dst, src=src: e.dma_start(out=dst, in_=src))
                    S.dma("pool", T_WS[sl], fns, reads=[], writes=[RWS[sl]])
                    issued += 1
                loads, compute = steps[i]
                S.label = "%s@%d" % (getattr(compute, "_lbl", "step"), i)
                if loads:
                    m = num_of[i]
                    assert m < issued
                    compute(WS[:, m % 3, :], RWS[m % 3])
                    c += 1
                else:
                    compute(None, None)
            steps.clear()

        def wview(sv, kdim, ncol):
            return sv[:, 0:kdim * ncol].rearrange("p (k n) -> p k n", k=kdim)

        def slab(si, kdim, ncol):
            return wsl_d[si][:, 0:kdim * ncol].rearrange("p (k n) -> p k n", k=kdim)

        def wsrc(w_ap, c0, ncol):
            return w_ap[:, c0:c0 + ncol].rearrange("(k p) n -> p k n", p=128)

        def simple_load(dst, src, res):
            S.dma("sp", ld_tok(), lambda e: e.dma_start(out=dst, in_=src), reads=[], writes=res)

        simple_load(VEC[:], vec_d, [R("VEC")])
        simple_load(CT[:], cT_d, [R("CT")])
        simple_load(IDF[:], ident_d, [R("IDF")])
        simple_load(MASKT[:], maskT_d, [R("MASKT")])
        simple_load(TRII[:], triI_d, [R("TRII")])
        simple_load(TRIR[:], triR_d, [R("TRIR")])
        simple_load(I16N[:], i16n_d, [R("I16N")])
        simple_load(ID16[:], id16_d, [R("ID16")])
        simple_load(GNB[:], gnb_d, [R("GNB")])
        simple_load(SC[:], sconv_d, [R("SC")])
        simple_load(TMP[0][0:17, :], walpha_d, [RTMP[0]])
        ACT(lambda e: e.activation(out=WAUG[:], in_=TMP[0][0:17, :], func=AF.Copy), [RTMP[0]], [R("WAUG")])
        DVE(lambda e: e.tensor_copy(out=IDB[:], in_=IDF[:]), [R("IDF")], [R("IDB")])
        DVE(lambda e: e.memset(ONESB[:], 1.0), [], [R("ONESB")])
        DVE(lambda e: e.memset(EPSD[:], EPS), [], [R("EPSD")])
        DVE(lambda e: e.memset(LNQ[:], math.log(128.0 ** -0.5)), [], [R("LNQ")])
        DVE(lambda e: e.memset(SST[:], 0.0), [], [R("SST")])
        DVE(lambda e: e.memset(SBF[:], 0.0), [], [R("SBF")])
        DVE(lambda e: e.memset(TAIL[:], 0.0), [], [R("TAIL")])
        ACT(lambda e: e.activation(out=USM[:, :, 0, :], in_=SC[:, :, 1, :], func=AF.Copy), [R("SC")], [R("USM")])
        ACT(lambda e: e.activation(out=SCT[:], in_=CT[:], func=AF.Silu), [R("CT")], [R("SCT")])

        ada_bank = {}

        macw = [0.5, 1.0, 0.5]

        def mod_a(i):
            sc_i = ADA[:, (i * 3 + 1) * 8:(i * 3 + 2) * 8, :]
            gpre = VEC[:, V_GPRE + i * 8:V_GPRE + (i + 1) * 8].unsqueeze(2).to_broadcast([128, 8, 17])
            DVE(lambda e: e.scalar_tensor_tensor(out=AALL[:, i], in0=sc_i, scalar=1.0, in1=gpre, op0=ALU.add, op1=ALU.mult),
                [R("ADAss%d" % i), R("VEC")], [R("AALL%d" % i)])

        def mod_g(i):
            gt_i = ADA[:, (i * 3 + 2) * 8:(i * 3 + 3) * 8, :]
            gpost = VEC[:, V_GPOST + i * 8:V_GPOST + (i + 1) * 8].unsqueeze(2).to_broadcast([128, 8, 17])
            DVE(lambda e: e.scalar_tensor_tensor(out=GALL[:, i], in0=gt_i, scalar=macw[i], in1=gpost, op0=ALU.mult, op1=ALU.mult),
                [R("ADAg%d" % i), R("VEC")], [R("GALL%d" % i)])

        def ada_step(q):
            g = q // 6
            r6 = q % 6
            part = "ss" if r6 < 4 else "g"
            first = r6 in (0, 4)
            last = r6 in (3, 5)
            j0 = 0 if part == "ss" else 16
            nj = 16 if part == "ss" else 8

            def compute(sv, rs):
                wv = wview(sv, KC, 512)
                key = (g, part)
                if first:
                    b, br = pbank()
                    pinned.add(b)
                    ada_bank[key] = (b, br)
                b, br = ada_bank[key]
                for jj in range(4):
                    j = (r6 * 4 + jj) - j0
                    o = psb(b)[:, j * 17:(j + 1) * 17]
                    PE([lambda e, o=o, kc=kc, jj=jj: e.matmul(o, lhsT=wv[:, kc, jj * 128:(jj + 1) * 128], rhs=SCT[:, kc, :],
                                                              start=(kc == 0), stop=(kc == KC - 1)) for kc in range(KC)],
                       [rs, R("SCT")], br)
                if last:
                    pin = psb(b)[:, 0:nj * 17].rearrange("p (j c) -> p j c", c=17)
                    c_lo = g * 24 + j0
                    bb = VEC[:, V_BADA + c_lo:V_BADA + c_lo + nj].unsqueeze(2).to_broadcast([128, nj, 17])
                    DVE(lambda e: e.tensor_tensor(out=ADA[:, c_lo:c_lo + nj, :], in0=pin, in1=bb, op=ALU.add),
                        br + [R("VEC")], [R("ADA%s%d" % (part, g))])
                    pinned.discard(b)
                    if part == "ss":
                        mod_a(g)
                    else:
                        mod_g(g)
            add_step([(lambda sv: wview(sv, KC, 512), slab(SL_ADA + q, KC, 512))], compute, "ada")

        side_q = []

        def queue_ada(qs):
            for q in qs:
                side_q.append(lambda q=q: ada_step(q))

        def pop_side(k=1):
            for _ in range(k):
                if side_q:
                    side_q.pop(0)()

        def SH(i):
            return ADA[:, (i * 3) * 8:(i * 3 + 1) * 8, :]

        def blocks_of(sbi):
            bl = [(0, 512, 0, False), (512, 512, 1, False)]
            if sbi == 0:
                bl.append((SBT, NS, 2, True))
            return bl

        def rstd_for(srcs, src_res, n, dim):
            b, br = pbank()
            for kc in range(KC):
                si = nxt("sq", 2)
                ACT(lambda e, si=si, kc=kc: e.activation(out=SQ[si][:, :n], in_=srcs[kc], func=AF.Square), [src_res[kc]], [RSQ[si]])
                PE(lambda e, si=si, kc=kc: e.matmul(psb(b)[:, :n], lhsT=ONESB[:], rhs=SQ[si][:, :n], start=(kc == 0), stop=(kc == KC - 1)),
                   [RSQ[si], R("ONESB")], br)
            ri = nxt("rstd", 2)
            ACT(lambda e: e.activation(out=RSTD[ri][:, :n], in_=psb(b)[:, :n], func=AF.Ln, scale=1.0 / dim, bias=EPSD[:]), br + [R("EPSD")], [RRSTD[ri]])
            ACT(lambda e: e.activation(out=RSTD[ri][:, :n], in_=RSTD[ri][:, :n], func=AF.Exp, scale=-0.5), [RRSTD[ri]], [RRSTD[ri]])
            return RSTD[ri][:, :n], RRSTD[ri]

        ssb = {}
        deferred = []

        def flush_deferred():
            while deferred:
                deferred.pop(0)()

        def out_epilogue(b, br, dc, blk):
            (c0, n, tb, is_s) = blk
            if dc == 0:
                sb_, sbr = pbank()
                pinned.add(sb_)
                ssb[tb] = (sb_, sbr)
            sb_, sbr = ssb[tb]
            while len(deferred) > 1:
                deferred.pop(0)()
            ACT(lambda e: e.activation(out=OUT[:, dc, c0:c0 + n], in_=psb(b)[:, :n], func=AF.Copy), br, [ROUT[dc][tb], R("WGB")])
            si = nxt("sq", 2)
            ACT(lambda e: e.activation(out=SQ[si][:, :n], in_=psb(b)[:, :n], func=AF.Square), br, [RSQ[si]])
            deferred.append(lambda: PE(lambda e: e.matmul(psb(sb_)[:, :n], lhsT=ONESB[:], rhs=SQ[si][:, :n], start=(dc == 0), stop=(dc == KC - 1)),
                                       [RSQ[si], R("ONESB")], sbr))

        def rstd_from_ssb(tb, n, dim):
            flush_deferred()
            sb_, sbr = ssb.pop(tb)
            ri = nxt("rstd", 2)
            ACT(lambda e: e.activation(out=RSTD[ri][:, :n], in_=psb(sb_)[:, :n], func=AF.Ln, scale=1.0 / dim, bias=EPSD[:]), sbr + [R("EPSD")], [RRSTD[ri]])
            ACT(lambda e: e.activation(out=RSTD[ri][:, :n], in_=RSTD[ri][:, :n], func=AF.Exp, scale=-0.5), [RRSTD[ri]], [RRSTD[ri]])
            pinned.discard(sb_)
            return RSTD[ri][:, :n], RRSTD[ri]

        def prenorm_blk(i, blk):
            for (c0, n, tb, is_s) in [blk]:
                srcs = [X[:, kc, c0:c0 + n] for kc in range(KC)]
                rs, rr = rstd_for(srcs, [RX[kc][tb] for kc in range(KC)], n, D)
                if not is_s:
                    for kc in range(KC):
                        ti = nxt("tmp", 3)
                        DVE(lambda e, kc=kc, ti=ti: e.scalar_tensor_tensor(out=TMP[ti][:, :n], in0=X[:, kc, c0:c0 + n], scalar=AALL[:, i, kc, 0:1], in1=rs,
                                                                           op0=ALU.mult, op1=ALU.mult),
                            [RX[kc][tb], rr, R("AALL%d" % i)], [RTMP[ti]])
                        DVE(lambda e, kc=kc, ti=ti: e.tensor_scalar(out=XN[:, kc, c0:c0 + n], in0=TMP[ti][:, :n], scalar1=SH(i)[:, kc, 0:1], scalar2=None, op0=ALU.add),
                            [RTMP[ti], R("ADAss%d" % i)], [RXN[kc][tb]])
                else:
                    ti = nxt("tmp", 3)
                    tv = TMP[ti][:, 0:KC * NS].rearrange("p (k n) -> p k n", k=KC)
                    xs = X[:, :, c0:c0 + n]
                    rb = rs.unsqueeze(1).to_broadcast([128, KC, NS])
                    allx = [RX[kc][tb] for kc in range(KC)]
                    DVE(lambda e: e.tensor_tensor(out=tv, in0=xs, in1=AALL[:, i, :, 1:17], op=ALU.mult), allx + [R("AALL%d" % i)], [RTMP[ti]])
                    DVE(lambda e: e.tensor_tensor(out=tv, in0=tv, in1=rb, op=ALU.mult), [RTMP[ti], rr], [RTMP[ti]])
                    DVE(lambda e: e.tensor_tensor(out=XN[:, :, c0:c0 + n], in0=tv, in1=SH(i)[:, :, 1:17], op=ALU.add), [RTMP[ti], R("ADAss%d" % i)],
                        [RXN[kc][tb] for kc in range(KC)])

        def post_blk(i, blk):
            for (c0, n, tb, is_s) in [blk]:
                rs, rr = rstd_from_ssb(tb, n, D)
                if not is_s:
                    for kc in range(KC):
                        DVE(lambda e, kc=kc: e.scalar_tensor_tensor(out=OUT[:, kc, c0:c0 + n], in0=OUT[:, kc, c0:c0 + n], scalar=GALL[:, i, kc, 0:1], in1=rs,
                                                                    op0=ALU.mult, op1=ALU.mult),
                            [ROUT[kc][tb], rr, R("GALL%d" % i)], [ROUT[kc][tb]])
                        (POOL if kc % 3 != 2 else DVE)(lambda e, kc=kc: e.tensor_tensor(out=X[:, kc, c0:c0 + n], in0=X[:, kc, c0:c0 + n], in1=OUT[:, kc, c0:c0 + n], op=ALU.add),
                                                       [ROUT[kc][tb], RX[kc][tb]], [RX[kc][tb]])
                else:
                    ti = nxt("tmp", 3)
                    tv = TMP[ti][:, 0:KC * NS].rearrange("p (k n) -> p k n", k=KC)
                    rb = rs.unsqueeze(1).to_broadcast([128, KC, NS])
                    allo = [ROUT[kc][tb] for kc in range(KC)]
                    allx = [RX[kc][tb] for kc in range(KC)]
                    DVE(lambda e: e.tensor_tensor(out=tv, in0=OUT[:, :, c0:c0 + n], in1=GALL[:, i, :, 1:17], op=ALU.mult), allo + [R("GALL%d" % i)], [RTMP[ti]])
                    DVE(lambda e: e.tensor_tensor(out=tv, in0=tv, in1=rb, op=ALU.mult), [RTMP[ti], rr], [RTMP[ti]])
                    DVE(lambda e: e.tensor_tensor(out=X[:, :, c0:c0 + n], in0=X[:, :, c0:c0 + n], in1=tv, op=ALU.add), [RTMP[ti]] + allx, allx)

        def mm8(o, wv, wc0, wn, rhs_fn, reads, bres, is_lhs_act=False):
            PE([lambda e, kc=kc: e.matmul(o, lhsT=wv[:, kc, wc0:wc0 + wn], rhs=rhs_fn(kc), start=(kc == 0), stop=(kc == KC - 1)) for kc in range(KC)],
               reads, bres)

        def ffn_steps(i, w_in_d, w_out_d, sbi, side_every=0, flush_side=False):
            bl = blocks_of(sbi)
            cnt = [0]

            def maybe_side():
                cnt[0] += 1
                if side_every and cnt[0] % side_every == (1 if side_every > 1 else 0):
                    pop_side()
            for fp in range(FC // 2):
                def compute(sv, rs, fp=fp):
                    wv = wview(sv, KC, 512)
                    for ff in range(2):
                        f = 2 * fp + ff
                        for (c0, n, tb, is_s) in bl:
                            xr = [RXN[kc][tb] for kc in range(KC)]
                            bg, rg = pbank()
                            mm8(psb(bg)[:, :n], wv, ff * 128, 128, lambda kc: XN[:, kc, c0:c0 + n], [rs] + xr, rg)
                            bu, ru = pbank()
                            mm8(psb(bu)[:, :n], wv, 256 + ff * 128, 128, lambda kc: XN[:, kc, c0:c0 + n], [rs] + xr, ru)
                            ti = nxt("tmp", 3)
                            ACT(lambda e, bg=bg, ti=ti, n=n: e.activation(out=TMP[ti][:, :n], in_=psb(bg)[:, :n], func=AF.Silu), rg, [RTMP[ti]])
                            DVE(lambda e, bu=bu, ti=ti, n=n, c0=c0, f=f: e.tensor_tensor(out=BIG[:, f, c0:c0 + n], in0=TMP[ti][:, :n], in1=psb(bu)[:, :n], op=ALU.mult),
                                [RTMP[ti]] + ru, [RBIG[f][tb]])
                add_step([(lambda sv: wview(sv, KC, 512), slab(w_in_d + fp, KC, 512))], compute, "ffnA%d" % i)
                maybe_side()
            for dc in range(KC):
                def compute(sv, rs, dc=dc):
                    wv = wview(sv, FC, 128)
                    for (c0, n, tb, is_s) in bl:
                        b, br = pbank()
                        PE([lambda e, f=f, b=b, n=n, c0=c0: e.matmul(psb(b)[:, :n], lhsT=wv[:, f, :], rhs=BIG[:, f, c0:c0 + n], start=(f == 0), stop=(f == FC - 1))
                            for f in range(FC)], [rs] + [RBIG[f][tb] for f in range(FC)], br)
                        out_epilogue(b, br, dc, (c0, n, tb, is_s))
                add_step([(lambda sv: wview(sv, FC, 128), slab(w_out_d + dc, FC, 128))], compute, "ffnB%d" % i)
                maybe_side()
            if flush_side:
                while side_q:
                    pop_side()

        def arena_views():
            c = state["c"]
            slots = [(c + 1) % 3, (c + 2) % 3]
            cur = [0, 0]

            def take_bf(nel):
                if cur[1] + nel > 4096:
                    cur[0] += 1
                    cur[1] = 0
                v = WS[:, slots[cur[0]], cur[1]:cur[1] + nel]
                cur[1] += nel
                return v

            def take_f32(nel):
                return take_bf(2 * nel).bitcast(F32)
            A = {}
            A["la"] = take_f32(512)
            A["e1"] = A["la"]
            A["eb"] = take_f32(512)
            A["enb"] = take_f32(512)
            A["ek"] = take_f32(512)
            A["qt"] = take_bf(512)
            A["kt"] = take_bf(512)
            A["kh"] = take_bf(512)
            A["vt"] = take_bf(1024)
            A["sc"] = take_bf(512)
            A["on"] = take_bf(1024)
            assert cur[0] <= 1 and cur[1] <= 4096
            A["sjb2"] = WS[:, slots[1], 0:1024]
            A["qmj2"] = WS[:, slots[1], 1024:1088]
            A["s0j"] = [S0J1[:], S0J2[:],
                        WS[:, slots[0], 0:2048].bitcast(F32).rearrange("p (h v) -> p h v", h=4),
                        WS[:, slots[0], 2048:4096].bitcast(F32).rearrange("p (h v) -> p h v", h=4)]
            A["s0j_res"] = [[R("S0J0")], [R("S0J1")], [R("la"), R("eb")], [R("enb"), R("ek")]]
            xo = [0]

            def take_x(nel):
                v = ARX[:, xo[0]:xo[0] + nel]
                xo[0] += nel
                return v
            A["junk"] = take_x(256)
            A["st"] = take_x(8)
            A["ebl"] = take_x(4)
            A["zaug"] = take_x(64).bitcast(BF16)
            A["ea"] = take_x(64)
            A["qs"] = take_x(64)
            A["qmj"] = take_x(32).bitcast(BF16)
            A["kmj"] = take_x(256).bitcast(BF16)
            A["ktok"] = take_x(512)
            A["sjb"] = take_x(512).bitcast(BF16)
            assert xo[0] <= 1792, xo[0]
            return A


        def gla_common_front(A, c0, n, tb):
            xr = [RXN[kc][tb] for kc in range(KC)]
            WGA = BIG[:, 16:24, :]
            WGB = OUT[:].rearrange("p k n -> p (k n)").bitcast(BF16)[:, 0:KC * 2048].rearrange("p (k n) -> p k n", k=KC)
            rwa = [R("WGA")]
            rwb = [R("WGB")]
            bz, rz = pbank()
            mm8(psb(bz)[0:16, :n], WGA, 1024, 16, lambda kc: XN[:, kc, c0:c0 + n], rwa + xr, rz)
            ACT(lambda e: e.activation(out=A["zaug"][0:16, :n], in_=psb(bz)[0:16, :n], func=AF.Copy), rz, [R("zaug")])
            bp, rp = pbank()
            PE(lambda e: e.matmul(psb(bp)[:n, :], lhsT=A["zaug"][0:17, :n], rhs=WAUG[0:17, :], start=True, stop=True), [R("zaug"), R("WAUG")], rp)
            ACT(lambda e: e.activation(out=A["e1"][:n, :], in_=psb(bp)[:n, :], func=AF.Exp, scale=-1.0), rp, [R("la")])
            ACT(lambda e: e.activation(out=A["la"][:n, :], in_=A["e1"][:n, :], func=AF.Ln, bias=1.0), [R("la")], [R("la")])
            return WGA, WGB, rwa, rwb, xr

        def gla_norm(A, po, rpo, c0, n, tb):
            st = A["st"]
            DVE(lambda e: e.memset(st[:, 0:8], 0.0), [], [R("st")])
            for h in range(4):
                ACT(lambda e, h=h: e.activation(out=A["junk"][:n, :], in_=po[:n, h * 256:(h + 1) * 256], func=AF.Square, accum_out=st[:n, h:h + 1]),
                    rpo + [R("st")], [R("junk"), R("st")])
            ACT(lambda e: e.activation(out=st[:n, 4:8], in_=st[:n, 0:4], func=AF.Ln, scale=1.0 / 256, bias=EPSD[:n, :]), [R("st"), R("EPSD")], [R("st")])
            ACT(lambda e: e.activation(out=st[:n, 4:8], in_=st[:n, 4:8], func=AF.Exp, scale=-0.5), [R("st")], [R("st")])
            for h in range(4):
                DVE(lambda e, h=h: e.scalar_tensor_tensor(out=A["on"][:n, h * 256:(h + 1) * 256], in0=po[:n, h * 256:(h + 1) * 256], scalar=st[:n, 4 + h:5 + h],
                                                          in1=GNB[:n, h * 256:(h + 1) * 256], op0=ALU.mult, op1=ALU.mult),
                    rpo + [R("st"), R("GNB")], [R("on")])

        def gla_back_T(A, c0, n, tb):
            bt, rt = pbank()
            ptv = psb(bt).bitcast(BF16)[:, 0:1024].rearrange("p (k n) -> p k n", k=KC)
            PE([lambda e, dc=dc: e.transpose(out=ptv[:, dc, :n], in_=A["on"][:n, dc * 128:(dc + 1) * 128], identity=IDB[:n, :n]) for dc in range(KC)],
               [R("on"), R("IDB")], rt)
            DVE(lambda e: e.tensor_tensor(out=BIG[:, 8:16, c0:c0 + n], in0=ptv[:, :, :n], in1=BIG[:, 8:16, c0:c0 + n], op=ALU.mult),
                rt + [RBIG[8 + k][tb] for k in range(KC)], [RBIG[8 + k][tb] for k in range(KC)])

        def gla_rproj(sbi):
            WGB = OUT[:].rearrange("p k n -> p (k n)").bitcast(BF16)[:, 0:KC * 2048].rearrange("p (k n) -> p k n", k=KC)
            for (c0, n, tb, is_s) in blocks_of(sbi):
                xr = [RXN[kc][tb] for kc in range(KC)]
                for dc in range(KC):
                    b, br = pbank()
                    mm8(psb(b)[:, :n], WGB, 1024 + dc * 128, 128, lambda kc: XN[:, kc, c0:c0 + n], [R("WGB")] + xr, br)
                    ACT(lambda e, b=b, dc=dc: e.activation(out=BIG[:, 8 + dc, c0:c0 + n], in_=psb(b)[:, :n], func=AF.Silu), br, [RBIG[8 + dc][tb]])

        def gla_zproj():
            WGA = BIG[:, 16:24, :]
            for tb in range(2):
                c0 = tb * 512
                zb = TMP[tb][:].bitcast(BF16)
                xr = [RXN[kc][tb] for kc in range(KC)]
                DVE(lambda e, zb=zb: e.memset(zb[0:32, 0:512], 1.0), [], [RTMP[tb]])
                bz, rz = pbank()
                mm8(psb(bz)[0:16, :], WGA, 1024, 16, lambda kc: XN[:, kc, c0:c0 + 512], [R("WGA")] + xr, rz)
                ACT(lambda e, zb=zb, bz=bz: e.activation(out=zb[0:16, 0:512], in_=psb(bz)[0:16, :], func=AF.Copy), rz, [RTMP[tb]])

        def gla_front_a(A, j):
            c0, n, tb = j * 128, 128, j // 4
            xr = [RXN[kc][tb] for kc in range(KC)]
            WGA = BIG[:, 16:24, :]
            WGB = OUT[:].rearrange("p k n -> p (k n)").bitcast(BF16)[:, 0:KC * 2048].rearrange("p (k n) -> p k n", k=KC)
            rwa = [R("WGA")]
            rwb = [R("WGB")]
            la = A["la"]
            zb = TMP[tb][:].bitcast(BF16)
            t0 = (j % 4) * 128
            bp, rp = pbank()
            PE(lambda e: e.matmul(psb(bp)[:n, :], lhsT=zb[0:17, t0:t0 + n], rhs=WAUG[0:17, :], start=True, stop=True), [RTMP[tb], R("WAUG")], rp)
            ACT(lambda e: e.activation(out=A["e1"][:n, :], in_=psb(bp)[:n, :], func=AF.Exp, scale=-1.0), rp, [R("la")])
            ACT(lambda e: e.activation(out=la[:n, :], in_=A["e1"][:n, :], func=AF.Ln, bias=1.0), [R("la")], [R("la")])

        def gla_front_b(A, j):
            c0, n, tb = j * 128, 128, j // 4
            xr = [RXN[kc][tb] for kc in range(KC)]
            WGA = BIG[:, 16:24, :]
            WGB = OUT[:].rearrange("p k n -> p (k n)").bitcast(BF16)[:, 0:KC * 2048].rearrange("p (k n) -> p k n", k=KC)
            rwa = [R("WGA")]
            rwb = [R("WGB")]
            la = A["la"]
            bvv, rvv = pbank(2)
            for half in range(2):
                PE([lambda e, kc=kc, half=half: e.matmul(psb(bvv + half), lhsT=XN[:, kc, c0:c0 + n], rhs=WGB[:, kc, half * 512:(half + 1) * 512],
                                                         start=(kc == 0), stop=(kc == KC - 1)) for kc in range(KC)], rwb + xr, [rvv[half]])
            DVE(lambda e: e.tensor_copy(out=A["vt"], in_=psb(bvv, 2)), rvv, [R("vt")])
            bb, rb = pbank()
            for h in range(4):
                PE(lambda e, h=h: e.matmul(psb(bb)[:, h * 128:(h + 1) * 128], lhsT=la[:, h * 128:(h + 1) * 128], rhs=TRII[:], start=True, stop=True),
                   [R("la"), R("TRII")], rb)
            bv, rv = pbank()
            PE(lambda e: e.matmul(psb(bv), lhsT=TRIR[:], rhs=la, start=True, stop=True), [R("la"), R("TRIR")], rv)
            bkt, rkt = pbank()
            PE([lambda e, kc=kc: e.matmul(psb(bkt), lhsT=XN[:, kc, c0:c0 + n], rhs=WGA[:, kc, 512:1024], start=(kc == 0), stop=(kc == KC - 1)) for kc in range(KC)],
               rwa + xr, rkt)
            ACT(lambda e: e.activation(out=A["eb"], in_=psb(bb), func=AF.Exp, bias=LNQ[:]), rb + [R("LNQ")], [R("eb")])
            ACT(lambda e: e.activation(out=A["enb"], in_=psb(bb), func=AF.Exp, scale=-1.0), rb, [R("enb")])
            pbl = psb(bb).rearrange("p (h t) -> p h t", h=4)[:, :, 127:128]
            ACT(lambda e: e.activation(out=A["ebl"].unsqueeze(2), in_=pbl, func=AF.Exp), rb, [R("ebl")])
            ACT(lambda e: e.activation(out=A["ek"], in_=psb(bv), func=AF.Exp), rv, [R("ek")])
            DVE(lambda e: e.tensor_tensor(out=A["kh"], in0=psb(bkt), in1=A["ek"], op=ALU.mult), rkt + [R("ek")], [R("kh")])
            bq, rq = pbank()
            for h in range(4):
                mm8(psb(bq)[:, h * 128:(h + 1) * 128], WGA, h * 128, 128, lambda kc: XN[:, kc, c0:c0 + n], rwa + xr, rq)
            DVE(lambda e: e.tensor_tensor(out=A["qt"], in0=psb(bq), in1=A["eb"], op=ALU.mult), rq + [R("eb")], [R("qt")])
            bk, rk = pbank()
            for h in range(4):
                mm8(psb(bk)[:, h * 128:(h + 1) * 128], WGA, 512 + h * 128, 128, lambda kc: XN[:, kc, c0:c0 + n], rwa + xr, rk)
            DVE(lambda e: e.tensor_tensor(out=A["kt"], in0=psb(bk), in1=A["enb"], op=ALU.mult), rk + [R("enb")], [R("kt")])

        def gla_mid(A, j):
            bs, rs_ = pbank()
            for h in range(4):
                PE(lambda e, h=h: e.matmul(psb(bs)[:, h * 128:(h + 1) * 128], lhsT=A["kt"][:, h * 128:(h + 1) * 128], rhs=A["qt"][:, h * 128:(h + 1) * 128],
                                           start=True, stop=True), [R("kt"), R("qt")], rs_)
            mb = MASKT[:].unsqueeze(1).to_broadcast([128, 4, 128])
            DVE(lambda e: e.tensor_tensor(out=A["sc"].rearrange("p (h t) -> p h t", h=4), in0=psb(bs).rearrange("p (h t) -> p h t", h=4), in1=mb, op=ALU.mult),
                rs_ + [R("MASKT")], [R("sc")])
            bd, rd = pbank(2)
            for h in range(4):
                PE(lambda e, h=h: e.matmul(psb(bd, 2)[:, h * 256:(h + 1) * 256], lhsT=A["kh"][:, h * 128:(h + 1) * 128], rhs=A["vt"][:, h * 256:(h + 1) * 256],
                                           start=True, stop=True), [R("kh"), R("vt")], [rd[h // 2]])
            bo, ro = pbank(2)
            for h in range(4):
                PE([lambda e, h=h: e.matmul(psb(bo, 2)[:, h * 256:(h + 1) * 256], lhsT=A["qt"][:, h * 128:(h + 1) * 128], rhs=SBF[:, h, :],
                                            start=True, stop=False),
                    lambda e, h=h: e.matmul(psb(bo, 2)[:, h * 256:(h + 1) * 256], lhsT=A["sc"][:, h * 128:(h + 1) * 128], rhs=A["vt"][:, h * 256:(h + 1) * 256],
                                            start=False, stop=True)],
                   [R("sc"), R("vt"), R("qt"), R("SBF")], [ro[h // 2]])
            for h in range(4):
                DVE(lambda e, h=h: e.scalar_tensor_tensor(out=SST[:, h, :], in0=SST[:, h, :], scalar=A["ebl"][:, h:h + 1], in1=psb(bd, 2)[:, h * 256:(h + 1) * 256],
                                                          op0=ALU.mult, op1=ALU.add), [R("SST"), R("ebl"), rd[h // 2]], [R("SST")])
            POOL(lambda e: e.tensor_copy(out=SBF[:], in_=SST[:]), [R("SST")], [R("SBF")])
            gla_norm(A, psb(bo, 2), ro, j * 128, 128, j // 4)

        def gla_sample(A):
            c0, n, tb = SBT, NS, 2
            WGA, WGB, rwa, rwb, xr = gla_common_front(A, c0, n, tb)
            la = A["la"]
            bb, rb = pbank()
            for h in range(4):
                PE(lambda e, h=h: e.matmul(psb(bb)[:, h * NS:(h + 1) * NS], lhsT=la[:NS, h * 128:(h + 1) * 128], rhs=I16N[:], start=True, stop=True),
                   [R("la"), R("I16N")], rb)
            ACT(lambda e: e.activation(out=A["ea"], in_=psb(bb)[:, 0:64], func=AF.Exp), rb, [R("ea")])
            bq, rq = pbank()
            for h in range(4):
                mm8(psb(bq)[:, h * NS:(h + 1) * NS], WGA, h * 128, 128, lambda kc: XN[:, kc, c0:c0 + n], rwa + xr, rq)
            ACT(lambda e: e.activation(out=A["qs"], in_=psb(bq)[:, 0:64], func=AF.Copy, scale=128.0 ** -0.5), rq, [R("qs")])
            bkt, rkt = pbank()
            PE([lambda e, kc=kc: e.matmul(psb(bkt)[:NS, :], lhsT=XN[:, kc, c0:c0 + n], rhs=WGA[:, kc, 512:1024], start=(kc == 0), stop=(kc == KC - 1)) for kc in range(KC)],
               rwa + xr, rkt)
            ACT(lambda e: e.activation(out=A["ktok"][:NS, :], in_=psb(bkt)[:NS, :], func=AF.Copy), rkt, [R("ktok")])
            bvv, rvv = pbank(2)
            for half in range(2):
                PE([lambda e, kc=kc, half=half: e.matmul(psb(bvv + half)[:NS, :], lhsT=XN[:, kc, c0:c0 + n], rhs=WGB[:, kc, half * 512:(half + 1) * 512],
                                                         start=(kc == 0), stop=(kc == KC - 1)) for kc in range(KC)], rwb + xr, [rvv[half]])
            ACT(lambda e: e.activation(out=A["vt"][:NS, :], in_=psb(bvv, 2)[:NS, :], func=AF.Copy), rvv, [R("vt")])
            bo, ro = pbank(2)
            pinned.add(bo)
            pinned.add(bo + 1)
            po = psb(bo, 2)
            qsv = A["qs"].rearrange("p (h t) -> p h t", h=4)
            qmvs = [A["qmj"].rearrange("p (h t) -> p h t", h=4), A["qmj2"].rearrange("p (h t) -> p h t", h=4)]
            rqm = [[R("qmj")], [R("kh")]]
            sjvs = [A["sjb"].rearrange("p (h v) -> p h v", h=4), A["sjb2"].rearrange("p (h v) -> p h v", h=4)]
            rsjb = [[R("sjb")], [R("qt"), R("kt")]]
            def load_state(j):
                sj = A["s0j"][j % 4]
                S.dma("sp", T_S0[j % 4], lambda e: e.dma_start(out=sj, in_=sgla_d[j].rearrange("h k v -> k h v")), reads=[], writes=A["s0j_res"][j % 4])
            load_state(0)
            load_state(1)
            load_state(2)
            eav = A["ea"].rearrange("p (h t) -> p h t", h=4)
            kmjs = [A["kmj"], A["junk"].bitcast(BF16)]
            rkm = [R("kmj"), R("junk")]
            dsb = {}

            def emit_ds(j):
                km = kmjs[j % 2]
                DVE(lambda e: e.tensor_scalar(out=km[:NS, :], in0=A["ktok"][:NS, :], scalar1=IDF[:NS, j:j + 1], scalar2=None, op0=ALU.mult),
                    [R("ktok"), R("IDF")], [rkm[j % 2]])
                bd, rd = pbank(2)
                for h in range(4):
                    PE(lambda e, h=h: e.matmul(psb(bd, 2)[:, h * 256:(h + 1) * 256], lhsT=km[:NS, h * 128:(h + 1) * 128], rhs=A["vt"][:NS, h * 256:(h + 1) * 256],
                                               start=True, stop=True), [rkm[j % 2], R("vt")], [rd[h // 2]])
                dsb[j] = (bd, rd)
            emit_ds(0)
            for j in range(NS):
                sj = A["s0j"][j % 4]
                rsjl = A["s0j_res"][j % 4]
                if j + 3 < NS:
                    load_state(j + 3)
                if j + 1 < NS:
                    emit_ds(j + 1)
                bd, rd = dsb.pop(j)
                for h in range(4):
                    DVE(lambda e, h=h: e.scalar_tensor_tensor(out=sj[:, h, :], in0=sj[:, h, :], scalar=eav[:, h, j:j + 1], in1=psb(bd, 2)[:, h * 256:(h + 1) * 256],
                                                              op0=ALU.mult, op1=ALU.add), rsjl + [R("ea"), rd[h // 2]], rsjl)
                S.dma("act", out_tok(), lambda e: e.dma_start(out=ogs_d[:, j * 4:(j + 1) * 4, :], in_=sj), reads=rsjl, writes=[R("ogs")])
                sjv = sjvs[j % 2]
                qmv = qmvs[j % 2]
                ACT(lambda e: e.activation(out=sjv, in_=sj, func=AF.Copy), rsjl, rsjb[j % 2])
                DVE(lambda e: e.tensor_tensor(out=qmv, in0=qsv, in1=ID16[:, j, :].unsqueeze(1).to_broadcast([128, 4, NS]), op=ALU.mult),
                    [R("qs"), R("ID16")], rqm[j % 2])
                for h in range(4):
                    PE(lambda e, h=h: e.matmul(po[:NS, h * 256:(h + 1) * 256], lhsT=qmv[:, h, :], rhs=sjv[:, h, :], start=(j == 0 and h % 2 == 0), stop=(j == NS - 1),
                                               skip_group_check=True), rqm[j % 2] + rsjb[j % 2], [ro[h // 2]])
            pinned.discard(bo)
            pinned.discard(bo + 1)
            gla_norm(A, po, ro, c0, n, tb)
            gla_back_T(A, c0, n, tb)

        def wga_load():
            WGA = BIG[:, 16:24, :]
            S.dma("pool", T_WGA, [lambda e: e.dma_start(out=WGA, in_=wga_d)], reads=[], writes=ALL_WGA + [R("WGA")])

        def mixer_steps(sbi):
            bl = blocks_of(sbi)
            WGA = BIG[:, 16:24, :]
            WGB = OUT[:].rearrange("p k n -> p (k n)").bitcast(BF16)[:, 0:KC * 2048].rearrange("p (k n) -> p k n", k=KC)

            def wgb_piece(p):
                WGB = OUT[:].rearrange("p k n -> p (k n)").bitcast(BF16)[:, 0:KC * 2048].rearrange("p (k n) -> p k n", k=KC)
                wr = (ALL_OUT + [R("WGB")]) if p == 0 else [R("WGB")]
                S.dma("pool", T_WGB, [lambda e: e.dma_start(out=WGB[:, :, p * 256:(p + 1) * 256], in_=wgb_d[:, :, p * 256:(p + 1) * 256])], reads=[], writes=wr)
            for c in range(KC):
                def compute(sv, rs, c=c):
                    wgb_piece(c)
                    wv = wview(sv, KC, 512)
                    wc = [VEC[:, V_WCONV + t * 8 + c:V_WCONV + t * 8 + c + 1] for t in range(3)]
                    ACT(lambda e: e.activation(out=UBUF[:, 0:2], in_=TAIL[:, c, :], func=AF.Copy), [R("TAIL")], [R("UBUF")])
                    for (c0, n, tb, is_s) in bl:
                        xr = [RXN[kc][tb] for kc in range(KC)]
                        bB, rB = pbank()
                        mm8(psb(bB)[:, :n], wv, 0, 128, lambda kc: XN[:, kc, c0:c0 + n], [rs] + xr, rB)
                        bC, rC = pbank()
                        mm8(psb(bC)[:, :n], wv, 128, 128, lambda kc: XN[:, kc, c0:c0 + n], [rs] + xr, rC)
                        bH, rH = pbank()
                        mm8(psb(bH)[:, :n], wv, 256, 128, lambda kc: XN[:, kc, c0:c0 + n], [rs] + xr, rH)
                        t0 = nxt("tmp", 3)
                        ACT(lambda e, bC=bC, t0=t0, n=n: e.activation(out=TMP[t0][:, :n], in_=psb(bC)[:, :n], func=AF.Copy), rC, [RTMP[t0]])
                        if not is_s:
                            DVE(lambda e, bH=bH, t0=t0, n=n, c0=c0: e.tensor_tensor(out=UBUF[:, 2 + c0:2 + c0 + n], in0=TMP[t0][:, :n], in1=psb(bH)[:, :n], op=ALU.mult),
                                [RTMP[t0]] + rH, [R("UBUF")])
                            t1 = nxt("tmp", 3)
                            ACT(lambda e, t1=t1, n=n, c0=c0: e.activation(out=TMP[t1][:, :n], in_=UBUF[:, c0:c0 + n], func=AF.Copy, scale=wc[0]), [R("UBUF"), R("VEC")], [RTMP[t1]])
                            DVE(lambda e, t1=t1, n=n, c0=c0: e.scalar_tensor_tensor(out=TMP[t1][:, :n], in0=UBUF[:, c0 + 1:c0 + 1 + n], scalar=wc[1], in1=TMP[t1][:, :n],
                                                                                    op0=ALU.mult, op1=ALU.add), [R("UBUF"), R("VEC"), RTMP[t1]], [RTMP[t1]])
                            DVE(lambda e, t1=t1, n=n, c0=c0: e.scalar_tensor_tensor(out=TMP[t1][:, :n], in0=UBUF[:, c0 + 2:c0 + 2 + n], scalar=wc[2], in1=TMP[t1][:, :n],
                                                                                    op0=ALU.mult, op1=ALU.add), [R("UBUF"), R("VEC"), RTMP[t1]], [RTMP[t1]])
                            DVE(lambda e, t1=t1, n=n, c0=c0, bB=bB: e.tensor_tensor(out=BIG[:, c, c0:c0 + n], in0=TMP[t1][:, :n], in1=psb(bB)[:, :n], op=ALU.mult),
                                [RTMP[t1]] + rB, [RBIG[c][tb]])
                        else:
                            DVE(lambda e, bH=bH, t0=t0: e.tensor_tensor(out=USM[:, c, 1, :], in0=TMP[t0][:, :NS], in1=psb(bH)[:, :NS], op=ALU.mult),
                                [RTMP[t0]] + rH, [R("USM")])
                            t1 = nxt("tmp", 3)
                            DVE(lambda e, t1=t1: e.tensor_scalar(out=TMP[t1][:, :NS], in0=SC[:, c, 0, :], scalar1=wc[0], scalar2=None, op0=ALU.mult), [R("SC"), R("VEC")], [RTMP[t1]])
                            DVE(lambda e, t1=t1: e.scalar_tensor_tensor(out=TMP[t1][:, :NS], in0=SC[:, c, 1, :], scalar=wc[1], in1=TMP[t1][:, :NS], op0=ALU.mult, op1=ALU.add),
                                [R("SC"), R("VEC"), RTMP[t1]], [RTMP[t1]])
                            DVE(lambda e, t1=t1: e.scalar_tensor_tensor(out=TMP[t1][:, :NS], in0=USM[:, c, 1, :], scalar=wc[2], in1=TMP[t1][:, :NS], op0=ALU.mult, op1=ALU.add),
                                [R("USM"), R("VEC"), RTMP[t1]], [RTMP[t1]])
                            DVE(lambda e, t1=t1, bB=bB: e.tensor_tensor(out=BIG[:, c, c0:c0 + NS], in0=TMP[t1][:, :NS], in1=psb(bB)[:, :NS], op=ALU.mult),
                                [RTMP[t1]] + rB, [RBIG[c][tb]])
                    ACT(lambda e: e.activation(out=TAIL[:, c, :], in_=UBUF[:, SBT:SBT + 2], func=AF.Copy), [R("UBUF")], [R("TAIL")])
                add_step([(lambda sv: wview(sv, KC, 512)[:, :, 0:384], slab(SL_CONV + c, KC, 512)[:, :, 0:384])], compute, "conv")
            def gla_all():
                S.barrier()
                A = arena_views()
                DVE(lambda e: e.memset(A["zaug"][0:32, :], 1.0), [], [R("zaug")])
                gla_rproj(sbi)
                gla_zproj()
                gla_front_a(A, 0)
                gla_front_b(A, 0)
                for j in range(8):
                    if j < 7:
                        gla_front_a(A, j + 1)
                    gla_mid(A, j)
                    if j < 7:
                        gla_front_b(A, j + 1)
                    gla_back_T(A, j * 128, 128, j // 4)
                if sbi == 0:
                    gla_sample(A)
                S.barrier()
            add_fence(0)
            add_work(gla_all, "gla")
            add_fence(1)
            for dc in range(KC):
                def compute(sv, rs, dc=dc):
                    wv = wview(sv, KC, 512)
                    for (c0, n, tb, is_s) in bl:
                        xr = [RXN[kc][tb] for kc in range(KC)]
                        b0, r0 = pbank()
                        mm8(psb(b0)[:, :n], wv, 0, 128, lambda kc: BIG[:, 8 + kc, c0:c0 + n], [rs] + [RBIG[8 + kc][tb] for kc in range(KC)], r0)
                        b1, r1 = pbank()
                        mm8(psb(b1)[:, :n], wv, 128, 128, lambda kc: BIG[:, kc, c0:c0 + n], [rs] + [RBIG[kc][tb] for kc in range(KC)], r1)
                        ba, ra = pbank()
                        mm8(psb(ba)[:, :n], wv, 256, 128, lambda kc: XN[:, kc, c0:c0 + n], [rs] + xr, ra)
                        bg, rg = pbank()
                        mm8(psb(bg)[:, :n], wv, 384, 128, lambda kc: XN[:, kc, c0:c0 + n], [rs] + xr, rg)
                        ta = nxt("tmp", 3)
                        tg = nxt("tmp", 3)
                        ACT(lambda e, ba=ba, ta=ta, n=n: e.activation(out=TMP[ta][:, :n], in_=psb(ba)[:, :n], func=AF.Sigmoid), ra, [RTMP[ta]])
                        ACT(lambda e, bg=bg, tg=tg, n=n: e.activation(out=TMP[tg][:, :n], in_=psb(bg)[:, :n], func=AF.Sigmoid), rg, [RTMP[tg]])
                        DVE(lambda e, b0=b0, ta=ta, n=n: e.tensor_tensor(out=TMP[ta][:, :n], in0=TMP[ta][:, :n], in1=psb(b0)[:, :n], op=ALU.mult), [RTMP[ta]] + r0, [RTMP[ta]])
                        DVE(lambda e, b1=b1, tg=tg, n=n: e.tensor_tensor(out=TMP[tg][:, :n], in0=TMP[tg][:, :n], in1=psb(b1)[:, :n], op=ALU.mult), [RTMP[tg]] + r1, [RTMP[tg]])
                        DVE(lambda e, ta=ta, tg=tg, n=n, c0=c0: e.tensor_tensor(out=BIG[:, 16 + dc, c0:c0 + n], in0=TMP[ta][:, :n], in1=TMP[tg][:, :n], op=ALU.add),
                            [RTMP[ta], RTMP[tg]], [RBIG[16 + dc][tb], R("WGA")])
                add_step([(lambda sv: wview(sv, KC, 512), slab(SL_MERGE + dc, KC, 512))], compute, "merge")
                if dc % 3 == 1:
                    pop_side()
            for q in range(2):
                def compute(sv, rs, q=q):
                    wv = wview(sv, KC, 512)
                    for dl in range(4):
                        dc = q * 4 + dl
                        for (c0, n, tb, is_s) in bl:
                            b, br = pbank()
                            mm8(psb(b)[:, :n], wv, dl * 128, 128, lambda kc: BIG[:, 16 + kc, c0:c0 + n], [rs] + [RBIG[16 + kc][tb] for kc in range(KC)], br)
                            out_epilogue(b, br, dc, (c0, n, tb, is_s))
                add_step([(lambda sv: wview(sv, KC, 512), slab(SL_OPROJ + q, KC, 512))], compute, "oproj")
            while side_q:
                pop_side()

        def swap_bufs():
            nonlocal X, OUT, RX, ROUT, ALL_OUT
            X, OUT = OUT, X
            RX, ROUT = ROUT, RX
            ALL_OUT = [r for row in ROUT for r in row]

        def next_sb_load(blk):
            swap_bufs()
            load_x_blk(1, blk)
            swap_bufs()

        def next_sb_prenorm(blk):
            swap_bufs()
            prenorm_blk(0, blk)
            swap_bufs()

        def load_x_blk(sbi, blk):
            (c0, n, tb, is_s) = blk
            if not is_s:
                for kc in range(KC):
                    S.dma("sp", ld_tok(), lambda e, kc=kc: e.dma_start(out=X[:, kc, c0:c0 + n], in_=xT_d[:, kc, sbi * SBT + c0:sbi * SBT + c0 + n]),
                          reads=[], writes=[RX[kc][tb]])
            else:
                S.dma("sp", ld_tok(), lambda e: e.dma_start(out=X[:, :, SBT:NT], in_=xT_d[:, :, SEQ:NCOL]), reads=[], writes=[RX[kc][2] for kc in range(KC)])

        def store_y_blk(sbi, blk):
            (c0, n, tb, is_s) = blk
            if not is_s:
                for kc in range(KC):
                    S.dma("sp", out_tok(), lambda e, kc=kc: e.dma_start(out=yT_d[:, kc, sbi * SBT + c0:sbi * SBT + c0 + n], in_=X[:, kc, c0:c0 + n]),
                          reads=[RX[kc][tb]], writes=[R("yT")])
            else:
                S.dma("sp", out_tok(), lambda e: e.dma_start(out=yT_d[:, :, SEQ:NCOL], in_=X[:, :, SBT:NT]), reads=[RX[kc][2] for kc in range(KC)], writes=[R("yT")])
                S.dma("sp", out_tok(), lambda e: e.dma_start(out=ocs_d, in_=USM[:]), reads=[R("USM")], writes=[R("ocs")])

        for q in range(4):
            ada_step(q)
        queue_ada([4, 5] + [6, 7, 8, 9] + [12, 13, 14, 15])
        for blk in blocks_of(0):
            add_work(lambda blk=blk: load_x_blk(0, blk))
            add_work(lambda blk=blk: prenorm_blk(0, blk), "prenorm_blk0")
        for sbi in range(2):
            ffn_steps(0, SL_F1A, SL_F1B, sbi, side_every=2 if sbi == 0 else 0, flush_side=True)
            if sbi == 0:
                queue_ada([10, 11])
            add_work(wga_load, "wga")
            for blk in blocks_of(sbi):
                add_work(lambda blk=blk: post_blk(0, blk), "post_blk0")
            for blk in blocks_of(sbi):
                add_work(lambda blk=blk: prenorm_blk(1, blk), "prenorm_blk1")
            mixer_steps(sbi)
            for blk in blocks_of(sbi):
                add_work(lambda blk=blk: post_blk(1, blk), "post_blk1")
            for blk in blocks_of(sbi):
                add_work(lambda blk=blk: prenorm_blk(2, blk), "prenorm_blk2")
            if sbi == 0:
                queue_ada([16, 17])
            ffn_steps(2, SL_F2A, SL_F2B, sbi, side_every=3 if sbi == 0 else 0, flush_side=True)
            for blk in blocks_of(sbi):
                add_work(lambda blk=blk: post_blk(2, blk), "post_blk2")
                add_work(lambda blk=blk, sbi=sbi: store_y_blk(sbi, blk))
                if sbi == 0 and not blk[3]:
                    add_work(lambda blk=blk: next_sb_load(blk), "load_next")
            if sbi == 0:
                for blk in blocks_of(1):
                    add_work(lambda blk=blk: next_sb_prenorm(blk), "prenorm_blk0")
                add_work(swap_bufs, "swap")
        run_steps()
        S.dma("sp", out_tok(), lambda e: e.dma_start(out=ogp_d, in_=SST[:]), reads=[R("SST")], writes=[R("ogp")])
        S.dma("sp", out_tok(), lambda e: e.dma_start(out=ocp_d, in_=TAIL[:]), reads=[R("TAIL")], writes=[R("ocp")])
        spE = S.engs["sp"]
        for t in T_OUT:
            S._wait(spE, t, t.count)
        S.barrier()

        with nc.Block() as block:
            def replay(en):
                def f(e):
                    for o in S.engs[en].ops:
                        if o[0] == "wait":
                            e.wait_ge(o[1].sem, o[2] * o[1].step)
                        else:
                            name, a, k = o[1]
                            ins = getattr(e, name)(*a, **k)
                            if o[2] is not None:
                                ins.then_inc(o[2].sem, o[2].step)
                return f
            block.tensor(replay("pe"))
            block.scalar(replay("act"))
            block.vector(replay("dve"))
            block.gpsimd(replay("pool"))
            block.sync(replay("sp"))
    build_nc.pe_labels = S.pe_labels
    return nc, list(dbg_d.keys())


_NC_CACHE = {}


def _consts():
    ident = np.eye(128, dtype=np.float32)
    s = np.arange(128)[:, None]
    t = np.arange(128)[None, :]
    maskT = (s <= t).astype(np.float32)
    triI = maskT * np.float32(-1.0 / 16.0)
    triR = (s > t).astype(np.float32) * np.float32(-1.0 / 16.0)
    i16n = np.eye(16, dtype=np.float32) * np.float32(-1.0 / 16.0)
    id16rep = np.ascontiguousarray(np.broadcast_to(np.eye(16, dtype=np.float32)[None], (128, 16, 16)))
    return dict(ident=ident, maskT=maskT, triI=triI.astype(np.float32), triR=triR.astype(np.float32), i16n=i16n, id16rep=id16rep)


def _slabs(w_ada, w1i, w1o, w2i, w2o, wmix, wbr, wmo):
    sl = np.zeros((NSL, 128, 4096), np.float32)

    def pkn(w, c0, n):
        K = w.shape[0]
        return w[:, c0:c0 + n].reshape(K // 128, 128, n).transpose(1, 0, 2)

    for q in range(18):
        sl[SL_ADA + q] = pkn(w_ada, q * 512, 512).reshape(128, -1)
    for (ba, bb, wi, wo) in ((SL_F1A, SL_F1B, w1i, w1o), (SL_F2A, SL_F2B, w2i, w2o)):
        for fp in range(11):
            t = np.concatenate([pkn(wi, fp * 256, 256), pkn(wi, DFF + fp * 256, 256)], axis=2)
            sl[ba + fp] = t.reshape(128, -1)
        for dc in range(8):
            sl[bb + dc, :, :FC * 128] = pkn(wo, dc * 128, 128).reshape(128, -1)
    z = np.zeros((128, KC, 128), np.float32)
    for c in range(8):
        t = np.concatenate([pkn(wmix, O_B + c * 128, 128), pkn(wmix, O_C + c * 128, 128), pkn(wmix, O_H + c * 128, 128), z], axis=2)
        sl[SL_CONV + c] = t.reshape(128, -1)
    for dc in range(8):
        t = np.concatenate([pkn(wbr[0], dc * 128, 128), pkn(wbr[1], dc * 128, 128), pkn(wmix, O_GA + dc * 128, 128), pkn(wmix, O_GB + dc * 128, 128)], axis=2)
        sl[SL_MERGE + dc] = t.reshape(128, -1)
    for q in range(2):
        sl[SL_OPROJ + q] = pkn(wmo, q * 512, 512).reshape(128, -1)
    wga = np.ascontiguousarray(np.concatenate([pkn(wmix, O_Q, 1024), pkn(wmix, O_Z, 16)], axis=2))
    wgb = np.ascontiguousarray(pkn(wmix, O_V, 2048))
    return sl, wga, wgb


def _fm(a):
    a = np.asarray(a, dtype=np.float32)
    lead = a.shape[:-1]
    a2 = a.reshape(-1, KC, 128)
    out = np.transpose(a2, (2, 1, 0))
    return np.ascontiguousarray(out.reshape((128, KC) + lead))


def kernel(x_prompt, x_sample, state_gla, state_conv, c_prompt, c_sample, w_ada, b_ada, g_pre, g_post,
           w_ffn1_in, w_ffn1_out, w_ffn2_in, w_ffn2_out, w_mix_in, w_alpha, b_alpha, g_gla_norm, w_conv,
           w_branch_out, w_mix_out, _dbg=()):
    f32 = lambda a: np.ascontiguousarray(np.asarray(a, dtype=np.float32))
    key = tuple(_dbg)
    if key not in _NC_CACHE:
        _NC_CACHE[key] = build_nc(_dbg)
    nc, dbg_names = _NC_CACHE[key]
    consts = _consts()
    x_prompt = f32(x_prompt); x_sample = f32(x_sample); state_gla = f32(state_gla); state_conv = f32(state_conv)
    c_prompt = f32(c_prompt); c_sample = f32(c_sample)
    vec = np.zeros((128, NVEC), np.float32)
    vec[:, V_BADA:V_BADA + 72] = f32(b_ada)[0].reshape(72, 128).T
    vec[:, V_GPRE:V_GPRE + 24] = f32(g_pre)[0].reshape(24, 128).T
    vec[:, V_GPOST:V_GPOST + 24] = f32(g_post)[0].reshape(24, 128).T
    vec[:, V_WCONV:V_WCONV + 24] = f32(w_conv)[0].reshape(24, 128).T
    gnb = np.ascontiguousarray(np.broadcast_to(f32(g_gla_norm)[0][None, :], (128, 1024)))
    walpha = np.ascontiguousarray(np.concatenate([f32(w_alpha)[0], f32(b_alpha)[0][None, :]], axis=0))
    wsl, wga, wgb = _slabs(f32(w_ada)[0], f32(w_ffn1_in)[0], f32(w_ffn1_out)[0], f32(w_ffn2_in)[0], f32(w_ffn2_out)[0],
                           f32(w_mix_in)[0], f32(w_branch_out)[0], f32(w_mix_out)[0])
    shared = dict(vecT=vec, gnb=gnb, walpha=walpha, wsl=wsl, wga=wga, wgb=wgb, **consts)
    in_maps = []
    for c in range(8):
        sl = slice(c * NS, (c + 1) * NS)
        xs = np.concatenate([x_prompt[c], x_sample[sl, 0, :]], axis=0)
        cc = np.concatenate([c_prompt[c:c + 1], c_sample[sl]], axis=0)
        m = dict(shared)
        m["xT"] = _fm(xs)
        m["cT"] = _fm(cc)
        m["sgla"] = np.ascontiguousarray(state_gla[0, sl])
        m["sconvT"] = np.ascontiguousarray(np.transpose(_fm(state_conv[0, sl]), (0, 1, 3, 2)))
        in_maps.append(m)
    res = run_bass_kernel_spmd(nc, in_maps, core_ids=list(range(8)))
    rs = res.results
    yp = np.zeros((8, SEQ, D), np.float32)
    ys = np.zeros((128, 1, D), np.float32)
    gp = np.zeros((1, 8, 4, 128, 256), np.float32)
    cp = np.zeros((1, 8, 2, D), np.float32)
    gs = np.zeros((1, 128, 4, 128, 256), np.float32)
    cs = np.zeros((1, 128, 2, D), np.float32)
    for c in range(8):
        r = rs[c]
        sl = slice(c * NS, (c + 1) * NS)
        yT = np.asarray(r["yT"])
        full = np.transpose(yT, (2, 1, 0)).reshape(NCOL, D)
        yp[c] = full[:SEQ]
        ys[sl, 0, :] = full[SEQ:]
        gp[0, c] = np.transpose(np.asarray(r["ogla_p"]), (1, 0, 2))
        cp[0, c] = np.transpose(np.asarray(r["oconv_p"]), (2, 1, 0)).reshape(2, D)
        gs[0, sl] = np.transpose(np.asarray(r["ogla_s"]).reshape(128, NS, 4, 256), (1, 2, 0, 3))
        cs[0, sl] = np.transpose(np.asarray(r["oconv_s"]), (3, 2, 1, 0)).reshape(NS, 2, D)
    if _dbg:
        kernel._dbg_out = [{n: np.asarray(rs[c]["dbg_" + n]) for n in dbg_names} for c in range(8)]
    return (yp, ys, gp, cp, gs, cs)
```

```python
import math
from contextlib import ExitStack

import numpy as np
import concourse.bass as bass
import concourse.mybir as mybir
from concourse.bass_utils import run_bass_kernel_spmd

F32 = mybir.dt.float32
BF16 = mybir.dt.bfloat16
AF = mybir.ActivationFunctionType
ALU = mybir.AluOpType

D = 1024
KC = 8
DFF = 2816
FC = 22
SEQ = 2048
NS = 16
SBT = 1024
NT = SBT + NS
NCOL = SEQ + NS
MIXW = 8208
O_Q, O_K, O_V, O_R, O_Z, O_B, O_C, O_H, O_GA, O_GB = 0, 512, 1024, 2048, 3072, 3088, 4112, 5136, 6160, 7184
EPS = 1e-6
V_BADA, V_GPRE, V_GPOST, V_WCONV = 0, 72, 96, 120
NVEC = 144
NSL = 74
SL_ADA, SL_F1A, SL_F1B, SL_CONV, SL_MERGE, SL_OPROJ, SL_F2A, SL_F2B = 0, 18, 29, 37, 45, 53, 55, 66


class Tok:
    def __init__(self, name, step):
        self.name = name
        self.step = step
        self.count = 0
        self.sem = None


class Eng(Tok):
    def __init__(self, name):
        super().__init__(name, 1)
        self.ops = []
        self.waited = {}


class Res:
    __slots__ = ("name", "last_w", "readers")

    def __init__(self, name):
        self.name = name
        self.last_w = None
        self.readers = []


class Rec:
    def __init__(self):
        self.calls = []

    def __getattr__(self, name):
        def f(*a, **k):
            self.calls.append((name, a, k))
            return self
        return f


def _rec(fn):
    r = Rec()
    fn(r)
    assert len(r.calls) == 1, r.calls
    return r.calls[0]


class Sched:
    def __init__(self):
        self.engs = {n: Eng(n) for n in ("pe", "act", "dve", "pool", "sp")}
        self.toks = list(self.engs.values())
        self.label = ""
        self.pe_labels = []

    def new_dma_tok(self, name):
        t = Tok(name, 16)
        self.toks.append(t)
        return t

    def _wait(self, E, tok, cnt):
        if cnt <= 0:
            return
        if E.waited.get(tok, 0) < cnt:
            E.waited[tok] = cnt
            E.ops.append(("wait", tok, cnt))

    def _deps(self, E, en, reads, writes):
        need = {}
        for r in reads:
            if r.last_w is not None:
                t, c = r.last_w
                if need.get(t, 0) < c:
                    need[t] = c
        for w in writes:
            if w.last_w is not None:
                t, c = w.last_w
                if need.get(t, 0) < c:
                    need[t] = c
            for (t, c) in w.readers:
                if need.get(t, 0) < c:
                    need[t] = c
        for t, c in need.items():
            if t is E and en == "pe":
                continue
            self._wait(E, t, c)

    def _record(self, me, reads, writes):
        for r in reads:
            r.readers.append(me)
            if len(r.readers) > 64:
                best = {}
                for (t, c) in r.readers:
                    if best.get(t, 0) < c:
                        best[t] = c
                r.readers = list(best.items())
        for w in writes:
            w.last_w = me
            w.readers = []

    def op(self, en, fns, reads=(), writes=()):
        E = self.engs[en]
        if not isinstance(fns, (list, tuple)):
            fns = [fns]
        self._deps(E, en, reads, writes)
        if en == "pe":
            self.pe_labels.extend([self.label] * len(fns))
        for f in fns[:-1]:
            E.ops.append(("ins", _rec(f), None))
        E.count += 1
        E.ops.append(("ins", _rec(fns[-1]), E))
        me = (E, E.count)
        self._record(me, reads, writes)
        return me

    def dma(self, en, tok, fns, reads=(), writes=()):
        E = self.engs[en]
        if not isinstance(fns, (list, tuple)):
            fns = [fns]
        self._wait(E, tok, tok.count)
        self._deps(E, en, reads, writes)
        for f in fns:
            tok.count += 1
            E.ops.append(("ins", _rec(f), tok))
        me = (tok, tok.count)
        self._record(me, reads, writes)
        return me

    def barrier(self):
        for E in self.engs.values():
            for t in self.toks:
                if t is E:
                    continue
                self._wait(E, t, t.count)


def build_nc(dbg_names=()):
    nc = bass.Bass("TRN2", target_bir_lowering=False)

    def din(name, shape):
        return nc.dram_tensor(name, list(shape), F32, kind="ExternalInput").ap()

    def dout(name, shape):
        return nc.dram_tensor(name, list(shape), F32, kind="ExternalOutput").ap()

    xT_d = din("xT", [128, KC, NCOL])
    cT_d = din("cT", [128, KC, 17])
    sgla_d = din("sgla", [NS, 4, 128, 256])
    sconv_d = din("sconvT", [128, KC, 2, NS])
    vec_d = din("vecT", [128, NVEC])
    gnb_d = din("gnb", [128, 1024])
    walpha_d = din("walpha", [17, 512])
    ident_d = din("ident", [128, 128])
    maskT_d = din("maskT", [128, 128])
    triI_d = din("triI", [128, 128])
    triR_d = din("triR", [128, 128])
    i16n_d = din("i16n", [16, 16])
    id16_d = din("id16rep", [128, 16, 16])
    wsl_d = din("wsl", [NSL, 128, 4096])
    wga_d = din("wga", [128, KC, 1040])
    wgb_d = din("wgb", [128, KC, 2048])

    yT_d = dout("yT", [128, KC, NCOL])
    ogp_d = dout("ogla_p", [128, 4, 256])
    ocp_d = dout("oconv_p", [128, KC, 2])
    ogs_d = dout("ogla_s", [128, NS * 4, 256])
    ocs_d = dout("oconv_s", [128, KC, 2, NS])
    dbg_d = {}

    S = Sched()
    es = ExitStack()
    with es:
        def sb(name, shape, dt):
            return es.enter_context(nc.sbuf_tensor(name, list(shape), dt))

        for n, E in S.engs.items():
            E.sem = es.enter_context(nc.semaphore("s_" + n))

        def dtok(name):
            t = S.new_dma_tok(name)
            t.sem = es.enter_context(nc.semaphore("d_" + name))
            return t

        X = sb("X", [128, KC, NT], F32)
        XN = sb("XN", [128, KC, NT], BF16)
        BIG = sb("BIG", [128, 24, NT], BF16)
        OUT = sb("OUT", [128, KC, NT], F32)
        WS = sb("WS", [128, 3, 4096], BF16)
        ARX = sb("ARX", [128, 1792], F32)
        ADA = sb("ADA", [128, 72, 17], F32)
        AALL = sb("AALL", [128, 3, KC, 17], F32)
        GALL = sb("GALL", [128, 3, KC, 17], F32)
        VEC = sb("VEC", [128, NVEC], F32)
        GNB = sb("GNB", [128, 1024], F32)
        IDF = sb("IDF", [128, 128], F32)
        IDB = sb("IDB", [128, 128], BF16)
        ONESB = sb("ONESB", [128, 128], BF16)
        MASKT = sb("MASKT", [128, 128], F32)
        TRII = sb("TRII", [128, 128], F32)
        TRIR = sb("TRIR", [128, 128], F32)
        I16N = sb("I16N", [16, 16], F32)
        ID16 = sb("ID16", [128, 16, 16], F32)
        SST = sb("SST", [128, 4, 256], F32)
        SBF = sb("SBF", [128, 4, 256], BF16)
        TAIL = sb("TAIL", [128, KC, 2], F32)
        SC = sb("SC", [128, KC, 2, NS], F32)
        USM = sb("USM", [128, KC, 2, NS], F32)
        WAUG = sb("WAUG", [17, 512], BF16)
        CT = sb("CT", [128, KC, 17], F32)
        SCT = sb("SCT", [128, KC, 17], BF16)
        EPSD = sb("EPSD", [128, 1], F32)
        LNQ = sb("LNQ", [128, 1], F32)
        RSTD = [sb("RSTD%d" % i, [128, 512], F32) for i in range(2)]
        SQ = [sb("SQ%d" % i, [128, 512], BF16) for i in range(2)]
        TMP = [sb("TMP%d" % i, [128, 512], F32) for i in range(3)]
        UBUF = ARX[:, 0:SBT + 2]
        S0J2 = sb("S0J2", [128, 4, 256], F32)
        S0J1 = sb("S0J1", [128, 4, 256], F32)
        PS = es.enter_context(nc.psum_tensor("PS", [128, 4096], F32))

        def psb(b, nb=1):
            return PS[:, b * 512:(b + nb) * 512]

        rdict = {}

        def R(name):
            r = rdict.get(name)
            if r is None:
                r = Res(name)
                rdict[name] = r
            return r

        RX = [[R("X%d_%d" % (k, t)) for t in range(3)] for k in range(KC)]
        RXN = [[R("XN%d_%d" % (k, t)) for t in range(3)] for k in range(KC)]
        RBIG = [[R("BIG%d_%d" % (k, t)) for t in range(3)] for k in range(24)]
        ROUT = [[R("OUT%d_%d" % (k, t)) for t in range(3)] for k in range(KC)]
        RWS = [R("WS%d" % i) for i in range(3)]
        RPS = [R("PS%d" % i) for i in range(8)]
        RRSTD = [R("RSTD%d" % i) for i in range(2)]
        RSQ = [R("SQ%d" % i) for i in range(2)]
        RTMP = [R("TMP%d" % i) for i in range(3)]
        ALL_OUT = [r for row in ROUT for r in row]
        ALL_WGA = [r for k in range(16, 24) for r in RBIG[k]]

        rot = {"ps": 0, "rstd": 0, "sq": 0, "tmp": 0, "ws": 0, "ld": 0}
        pinned = set()

        def pbank(nb=1):
            while True:
                b = rot["ps"] % 8
                if nb == 2 and b % 2 == 1:
                    rot["ps"] += 1
                    continue
                if any(((b + i) % 8) in pinned for i in range(nb)) or b + nb > 8:
                    rot["ps"] += 1
                    continue
                rot["ps"] += nb
                return b, [RPS[b + i] for i in range(nb)]

        def nxt(kind, n):
            i = rot[kind] % n
            rot[kind] += 1
            return i

        T_WS = [dtok("ws%d" % i) for i in range(3)]
        T_LD = [dtok("ld%d" % i) for i in range(6)]
        T_WGA = dtok("wga")
        T_WGB = dtok("wgb")
        T_OUT = [dtok("out%d" % i) for i in range(4)]
        T_S0 = [dtok("s0_%d" % i) for i in range(4)]
        rot["out"] = 0

        def ld_tok():
            return T_LD[nxt("ld", 6)]

        def out_tok():
            return T_OUT[nxt("out", 4)]

        PE = lambda fns, r=(), w=(): S.op("pe", fns, r, w)
        ACT = lambda fns, r=(), w=(): S.op("act", fns, r, w)
        DVE = lambda fns, r=(), w=(): S.op("dve", fns, r, w)
        POOL = lambda fns, r=(), w=(): S.op("pool", fns, r, w)

        def dbg(name, ap, reads):
            if name not in dbg_names:
                return
            shape = list(ap.shape)
            d = nc.dram_tensor("dbg_" + name, shape, ap.dtype, kind="ExternalOutput").ap()
            dbg_d[name] = d
            S.dma("sp", out_tok(), lambda e: e.dma_start(out=d, in_=ap), reads=reads, writes=[R("dbgout_" + name)])

        state = {"c": 0}
        steps = []
        FENCE = "fence"

        def add_step(loads, compute, lbl="step"):
            compute._lbl = lbl
            steps.append((loads, compute))

        def add_work(fn, lbl="work"):
            f = lambda sv, rs: fn()
            f._lbl = lbl
            steps.append(([], f))

        def add_fence(allow=0):
            steps.append((FENCE, allow))

        def run_steps():
            n = len(steps)
            is_f = lambda st: st[0] is FENCE
            load_idx = [i for i in range(n) if not is_f(steps[i]) and steps[i][0]]
            fences = [i for i in range(n) if is_f(steps[i])]
            num_of = {li: m for m, li in enumerate(load_idx)}
            issued = 0
            c = 0
            for i in range(n):
                if is_f(steps[i]):
                    continue
                nf = n
                allow = 0
                for f in fences:
                    if f > i:
                        nf = f
                        allow = steps[f][1]
                        break
                state["c"] = c

                def can_issue():
                    if issued >= len(load_idx) or issued > c + 2:
                        return False
                    li = load_idx[issued]
                    if li < nf:
                        return True
                    beyond = sum(1 for m in range(issued) if load_idx[m] > nf)
                    return beyond < allow
                while can_issue():
                    loads, _ = steps[load_idx[issued]]
                    sl = issued % 3
                    sv = WS[:, sl, :]
                    fns = []
                    for (dst_fn, src) in loads:
                        dst = dst_fn(sv)
                        fns.append(lambda e, dst=dst, src=src: e.dma_start(out=dst, in_=src))
                    S.dma("pool", T_WS[sl], fns, reads=[], writes=[RWS[sl]])
                    issued += 1
                loads, compute = steps[i]
                S.label = "%s@%d" % (getattr(compute, "_lbl", "step"), i)
                if loads:
                    m = num_of[i]
                    assert m < issued
                    compute(WS[:, m % 3, :], RWS[m % 3])
                    c += 1
                else:
                    compute(None, None)
            steps.clear()

        def wview(sv, kdim, ncol):
            return sv[:, 0:kdim * ncol].rearrange("p (k n) -> p k n", k=kdim)

        def slab(si, kdim, ncol):
            return wsl_d[si][:, 0:kdim * ncol].rearrange("p (k n) -> p k n", k=kdim)

        def wsrc(w_ap, c0, ncol):
            return w_ap[:, c0:c0 + ncol].rearrange("(k p) n -> p k n", p=128)

        def simple_load(dst, src, res):
            S.dma("sp", ld_tok(), lambda e: e.dma_start(out=dst, in_=src), reads=[], writes=res)

        simple_load(VEC[:], vec_d, [R("VEC")])
        simple_load(CT[:], cT_d, [R("CT")])
        simple_load(IDF[:], ident_d, [R("IDF")])
        simple_load(MASKT[:], maskT_d, [R("MASKT")])
        simple_load(TRII[:], triI_d, [R("TRII")])
        simple_load(TRIR[:], triR_d, [R("TRIR")])
        simple_load(I16N[:], i16n_d, [R("I16N")])
        simple_load(ID16[:], id16_d, [R("ID16")])
        simple_load(GNB[:], gnb_d, [R("GNB")])
        simple_load(SC[:], sconv_d, [R("SC")])
        simple_load(TMP[0][0:17, :], walpha_d, [RTMP[0]])
        ACT(lambda e: e.activation(out=WAUG[:], in_=TMP[0][0:17, :], func=AF.Copy), [RTMP[0]], [R("WAUG")])
        DVE(lambda e: e.tensor_copy(out=IDB[:], in_=IDF[:]), [R("IDF")], [R("IDB")])
        DVE(lambda e: e.memset(ONESB[:], 1.0), [], [R("ONESB")])
        DVE(lambda e: e.memset(EPSD[:], EPS), [], [R("EPSD")])
        DVE(lambda e: e.memset(LNQ[:], math.log(128.0 ** -0.5)), [], [R("LNQ")])
        DVE(lambda e: e.memset(SST[:], 0.0), [], [R("SST")])
        DVE(lambda e: e.memset(SBF[:], 0.0), [], [R("SBF")])
        DVE(lambda e: e.memset(TAIL[:], 0.0), [], [R("TAIL")])
        ACT(lambda e: e.activation(out=USM[:, :, 0, :], in_=SC[:, :, 1, :], func=AF.Copy), [R("SC")], [R("USM")])
        ACT(lambda e: e.activation(out=SCT[:], in_=CT[:], func=AF.Silu), [R("CT")], [R("SCT")])

        ada_bank = {}

        macw = [0.5, 1.0, 0.5]

        def mod_a(i):
            sc_i = ADA[:, (i * 3 + 1) * 8:(i * 3 + 2) * 8, :]
            gpre = VEC[:, V_GPRE + i * 8:V_GPRE + (i + 1) * 8].unsqueeze(2).to_broadcast([128, 8, 17])
            DVE(lambda e: e.scalar_tensor_tensor(out=AALL[:, i], in0=sc_i, scalar=1.0, in1=gpre, op0=ALU.add, op1=ALU.mult),
                [R("ADAss%d" % i), R("VEC")], [R("AALL%d" % i)])

        def mod_g(i):
            gt_i = ADA[:, (i * 3 + 2) * 8:(i * 3 + 3) * 8, :]
            gpost = VEC[:, V_GPOST + i * 8:V_GPOST + (i + 1) * 8].unsqueeze(2).to_broadcast([128, 8, 17])
            DVE(lambda e: e.scalar_tensor_tensor(out=GALL[:, i], in0=gt_i, scalar=macw[i], in1=gpost, op0=ALU.mult, op1=ALU.mult),
                [R("ADAg%d" % i), R("VEC")], [R("GALL%d" % i)])

        def ada_step(q):
            g = q // 6
            r6 = q % 6
            part = "ss" if r6 < 4 else "g"
            first = r6 in (0, 4)
            last = r6 in (3, 5)
            j0 = 0 if part == "ss" else 16
            nj = 16 if part == "ss" else 8

            def compute(sv, rs):
                wv = wview(sv, KC, 512)
                key = (g, part)
                if first:
                    b, br = pbank()
                    pinned.add(b)
                    ada_bank[key] = (b, br)
                b, br = ada_bank[key]
                for jj in range(4):
                    j = (r6 * 4 + jj) - j0
                    o = psb(b)[:, j * 17:(j + 1) * 17]
                    PE([lambda e, o=o, kc=kc, jj=jj: e.matmul(o, lhsT=wv[:, kc, jj * 128:(jj + 1) * 128], rhs=SCT[:, kc, :],
                                                              start=(kc == 0), stop=(kc == KC - 1)) for kc in range(KC)],
                       [rs, R("SCT")], br)
                if last:
                    pin = psb(b)[:, 0:nj * 17].rearrange("p (j c) -> p j c", c=17)
                    c_lo = g * 24 + j0
                    bb = VEC[:, V_BADA + c_lo:V_BADA + c_lo + nj].unsqueeze(2).to_broadcast([128, nj, 17])
                    DVE(lambda e: e.tensor_tensor(out=ADA[:, c_lo:c_lo + nj, :], in0=pin, in1=bb, op=ALU.add),
                        br + [R("VEC")], [R("ADA%s%d" % (part, g))])
                    pinned.discard(b)
                    if part == "ss":
                        mod_a(g)
                    else:
                        mod_g(g)
            add_step([(lambda sv: wview(sv, KC, 512), slab(SL_ADA + q, KC, 512))], compute, "ada")

        side_q = []

        def queue_ada(qs):
            for q in qs:
                side_q.append(lambda q=q: ada_step(q))

        def pop_side(k=1):
            for _ in range(k):
                if side_q:
                    side_q.pop(0)()

        def SH(i):
            return ADA[:, (i * 3) * 8:(i * 3 + 1) * 8, :]

        def blocks_of(sbi):
            bl = [(0, 512, 0, False), (512, 512, 1, False)]
            if sbi == 0:
                bl.append((SBT, NS, 2, True))
            return bl

        def rstd_for(srcs, src_res, n, dim):
            b, br = pbank()
            for kc in range(KC):
                si = nxt("sq", 2)
                ACT(lambda e, si=si, kc=kc: e.activation(out=SQ[si][:, :n], in_=srcs[kc], func=AF.Square), [src_res[kc]], [RSQ[si]])
                PE(lambda e, si=si, kc=kc: e.matmul(psb(b)[:, :n], lhsT=ONESB[:], rhs=SQ[si][:, :n], start=(kc == 0), stop=(kc == KC - 1)),
                   [RSQ[si], R("ONESB")], br)
            ri = nxt("rstd", 2)
            ACT(lambda e: e.activation(out=RSTD[ri][:, :n], in_=psb(b)[:, :n], func=AF.Ln, scale=1.0 / dim, bias=EPSD[:]), br + [R("EPSD")], [RRSTD[ri]])
            ACT(lambda e: e.activation(out=RSTD[ri][:, :n], in_=RSTD[ri][:, :n], func=AF.Exp, scale=-0.5), [RRSTD[ri]], [RRSTD[ri]])
            return RSTD[ri][:, :n], RRSTD[ri]

        ssb = {}
        deferred = []

        def flush_deferred():
            while deferred:
                deferred.pop(0)()

        def out_epilogue(b, br, dc, blk):
            (c0, n, tb, is_s) = blk
            if dc == 0:
                sb_, sbr = pbank()
                pinned.add(sb_)
                ssb[tb] = (sb_, sbr)
            sb_, sbr = ssb[tb]
            while len(deferred) > 1:
                deferred.pop(0)()
            ACT(lambda e: e.activation(out=OUT[:, dc, c0:c0 + n], in_=psb(b)[:, :n], func=AF.Copy), br, [ROUT[dc][tb], R("WGB")])
            si = nxt("sq", 2)
            ACT(lambda e: e.activation(out=SQ[si][:, :n], in_=psb(b)[:, :n], func=AF.Square), br, [RSQ[si]])
            deferred.append(lambda: PE(lambda e: e.matmul(psb(sb_)[:, :n], lhsT=ONESB[:], rhs=SQ[si][:, :n], start=(dc == 0), stop=(dc == KC - 1)),
                                       [RSQ[si], R("ONESB")], sbr))

        def rstd_from_ssb(tb, n, dim):
            flush_deferred()
            sb_, sbr = ssb.pop(tb)
            ri = nxt("rstd", 2)
            ACT(lambda e: e.activation(out=RSTD[ri][:, :n], in_=psb(sb_)[:, :n], func=AF.Ln, scale=1.0 / dim, bias=EPSD[:]), sbr + [R("EPSD")], [RRSTD[ri]])
            ACT(lambda e: e.activation(out=RSTD[ri][:, :n], in_=RSTD[ri][:, :n], func=AF.Exp, scale=-0.5), [RRSTD[ri]], [RRSTD[ri]])
            pinned.discard(sb_)
            return RSTD[ri][:, :n], RRSTD[ri]

        def prenorm_blk(i, blk):
            for (c0, n, tb, is_s) in [blk]:
                srcs = [X[:, kc, c0:c0 + n] for kc in range(KC)]
                rs, rr = rstd_for(srcs, [RX[kc][tb] for kc in range(KC)], n, D)
                if not is_s:
                    for kc in range(KC):
                        ti = nxt("tmp", 3)
                        DVE(lambda e, kc=kc, ti=ti: e.scalar_tensor_tensor(out=TMP[ti][:, :n], in0=X[:, kc, c0:c0 + n], scalar=AALL[:, i, kc, 0:1], in1=rs,
                                                                           op0=ALU.mult, op1=ALU.mult),
                            [RX[kc][tb], rr, R("AALL%d" % i)], [RTMP[ti]])
                        DVE(lambda e, kc=kc, ti=ti: e.tensor_scalar(out=XN[:, kc, c0:c0 + n], in0=TMP[ti][:, :n], scalar1=SH(i)[:, kc, 0:1], scalar2=None, op0=ALU.add),
                            [RTMP[ti], R("ADAss%d" % i)], [RXN[kc][tb]])
                else:
                    ti = nxt("tmp", 3)
                    tv = TMP[ti][:, 0:KC * NS].rearrange("p (k n) -> p k n", k=KC)
                    xs = X[:, :, c0:c0 + n]
                    rb = rs.unsqueeze(1).to_broadcast([128, KC, NS])
                    allx = [RX[kc][tb] for kc in range(KC)]
                    DVE(lambda e: e.tensor_tensor(out=tv, in0=xs, in1=AALL[:, i, :, 1:17], op=ALU.mult), allx + [R("AALL%d" % i)], [RTMP[ti]])
                    DVE(lambda e: e.tensor_tensor(out=tv, in0=tv, in1=rb, op=ALU.mult), [RTMP[ti], rr], [RTMP[ti]])
                    DVE(lambda e: e.tensor_tensor(out=XN[:, :, c0:c0 + n], in0=tv, in1=SH(i)[:, :, 1:17], op=ALU.add), [RTMP[ti], R("ADAss%d" % i)],
                        [RXN[kc][tb] for kc in range(KC)])

        def post_blk(i, blk):
            for (c0, n, tb, is_s) in [blk]:
                rs, rr = rstd_from_ssb(tb, n, D)
                if not is_s:
                    for kc in range(KC):
                        ti = nxt("tmp", 3)
                        DVE(lambda e, kc=kc, ti=ti: e.scalar_tensor_tensor(out=TMP[ti][:, :n], in0=OUT[:, kc, c0:c0 + n], scalar=GALL[:, i, kc, 0:1], in1=rs,
                                                                           op0=ALU.mult, op1=ALU.mult),
                            [ROUT[kc][tb], rr, R("GALL%d" % i)], [RTMP[ti]])
                        (POOL if kc % 3 != 2 else DVE)(lambda e, kc=kc, ti=ti: e.tensor_tensor(out=X[:, kc, c0:c0 + n], in0=X[:, kc, c0:c0 + n], in1=TMP[ti][:, :n], op=ALU.add),
                                                           [RTMP[ti], RX[kc][tb]], [RX[kc][tb]])
                else:
                    ti = nxt("tmp", 3)
                    tv = TMP[ti][:, 0:KC * NS].rearrange("p (k n) -> p k n", k=KC)
                    rb = rs.unsqueeze(1).to_broadcast([128, KC, NS])
                    allo = [ROUT[kc][tb] for kc in range(KC)]
                    allx = [RX[kc][tb] for kc in range(KC)]
                    DVE(lambda e: e.tensor_tensor(out=tv, in0=OUT[:, :, c0:c0 + n], in1=GALL[:, i, :, 1:17], op=ALU.mult), allo + [R("GALL%d" % i)], [RTMP[ti]])
                    DVE(lambda e: e.tensor_tensor(out=tv, in0=tv, in1=rb, op=ALU.mult), [RTMP[ti], rr], [RTMP[ti]])
                    DVE(lambda e: e.tensor_tensor(out=X[:, :, c0:c0 + n], in0=X[:, :, c0:c0 + n], in1=tv, op=ALU.add), [RTMP[ti]] + allx, allx)

        def mm8(o, wv, wc0, wn, rhs_fn, reads, bres, is_lhs_act=False):
            PE([lambda e, kc=kc: e.matmul(o, lhsT=wv[:, kc, wc0:wc0 + wn], rhs=rhs_fn(kc), start=(kc == 0), stop=(kc == KC - 1)) for kc in range(KC)],
               reads, bres)

        def ffn_steps(i, w_in_d, w_out_d, sbi, side_every=0, flush_side=False):
            bl = blocks_of(sbi)
            cnt = [0]

            def maybe_side():
                cnt[0] += 1
                if side_every and cnt[0] % side_every == (1 if side_every > 1 else 0):
                    pop_side()
            for fp in range(FC // 2):
                def compute(sv, rs, fp=fp):
                    wv = wview(sv, KC, 512)
                    for ff in range(2):
                        f = 2 * fp + ff
                        for (c0, n, tb, is_s) in bl:
                            xr = [RXN[kc][tb] for kc in range(KC)]
                            bg, rg = pbank()
                            mm8(psb(bg)[:, :n], wv, ff * 128, 128, lambda kc: XN[:, kc, c0:c0 + n], [rs] + xr, rg)
                            bu, ru = pbank()
                            mm8(psb(bu)[:, :n], wv, 256 + ff * 128, 128, lambda kc: XN[:, kc, c0:c0 + n], [rs] + xr, ru)
                            ti = nxt("tmp", 3)
                            ACT(lambda e, bg=bg, ti=ti, n=n: e.activation(out=TMP[ti][:, :n], in_=psb(bg)[:, :n], func=AF.Silu), rg, [RTMP[ti]])
                            DVE(lambda e, bu=bu, ti=ti, n=n, c0=c0, f=f: e.tensor_tensor(out=BIG[:, f, c0:c0 + n], in0=TMP[ti][:, :n], in1=psb(bu)[:, :n], op=ALU.mult),
                                [RTMP[ti]] + ru, [RBIG[f][tb]])
                add_step([(lambda sv: wview(sv, KC, 512), slab(w_in_d + fp, KC, 512))], compute, "ffnA%d" % i)
                maybe_side()
            for dc in range(KC):
                def compute(sv, rs, dc=dc):
                    wv = wview(sv, FC, 128)
                    for (c0, n, tb, is_s) in bl:
                        b, br = pbank()
                        PE([lambda e, f=f, b=b, n=n, c0=c0: e.matmul(psb(b)[:, :n], lhsT=wv[:, f, :], rhs=BIG[:, f, c0:c0 + n], start=(f == 0), stop=(f == FC - 1))
                            for f in range(FC)], [rs] + [RBIG[f][tb] for f in range(FC)], br)
                        out_epilogue(b, br, dc, (c0, n, tb, is_s))
                add_step([(lambda sv: wview(sv, FC, 128), slab(w_out_d + dc, FC, 128))], compute, "ffnB%d" % i)
                maybe_side()
            if flush_side:
                while side_q:
                    pop_side()

        def arena_views():
            c = state["c"]
            slots = [(c + 1) % 3, (c + 2) % 3]
            cur = [0, 0]

            def take_bf(nel):
                if cur[1] + nel > 4096:
                    cur[0] += 1
                    cur[1] = 0
                v = WS[:, slots[cur[0]], cur[1]:cur[1] + nel]
                cur[1] += nel
                return v

            def take_f32(nel):
                return take_bf(2 * nel).bitcast(F32)
            A = {}
            A["la"] = take_f32(512)
            A["e1"] = A["la"]
            A["eb"] = take_f32(512)
            A["enb"] = take_f32(512)
            A["ek"] = take_f32(512)
            A["qt"] = take_bf(512)
            A["kt"] = take_bf(512)
            A["kh"] = take_bf(512)
            A["vt"] = take_bf(1024)
            A["sc"] = take_bf(512)
            A["on"] = take_bf(1024)
            assert cur[0] <= 1 and cur[1] <= 4096
            A["sjb2"] = WS[:, slots[1], 0:1024]
            A["qmj2"] = WS[:, slots[1], 1024:1088]
            A["s0j"] = [S0J1[:], S0J2[:],
                        WS[:, slots[0], 0:2048].bitcast(F32).rearrange("p (h v) -> p h v", h=4),
                        WS[:, slots[0], 2048:4096].bitcast(F32).rearrange("p (h v) -> p h v", h=4)]
            A["s0j_res"] = [[R("S0J0")], [R("S0J1")], [R("la"), R("eb")], [R("enb"), R("ek")]]
            xo = [0]

            def take_x(nel):
                v = ARX[:, xo[0]:xo[0] + nel]
                xo[0] += nel
                return v
            A["junk"] = take_x(256)
            A["st"] = take_x(8)
            A["ebl"] = take_x(4)
            A["zaug"] = take_x(64).bitcast(BF16)
            A["ea"] = take_x(64)
            A["qs"] = take_x(64)
            A["qmj"] = take_x(32).bitcast(BF16)
            A["kmj"] = take_x(256).bitcast(BF16)
            A["ktok"] = take_x(512)
            A["sjb"] = take_x(512).bitcast(BF16)
            assert xo[0] <= 1792, xo[0]
            return A


        def gla_common_front(A, c0, n, tb):
            xr = [RXN[kc][tb] for kc in range(KC)]
            WGA = BIG[:, 16:24, :]
            WGB = OUT[:].rearrange("p k n -> p (k n)").bitcast(BF16)[:, 0:KC * 2048].rearrange("p (k n) -> p k n", k=KC)
            rwa = [R("WGA")]
            rwb = [R("WGB")]
            bz, rz = pbank()
            mm8(psb(bz)[0:16, :n], WGA, 1024, 16, lambda kc: XN[:, kc, c0:c0 + n], rwa + xr, rz)
            ACT(lambda e: e.activation(out=A["zaug"][0:16, :n], in_=psb(bz)[0:16, :n], func=AF.Copy), rz, [R("zaug")])
            bp, rp = pbank()
            PE(lambda e: e.matmul(psb(bp)[:n, :], lhsT=A["zaug"][0:17, :n], rhs=WAUG[0:17, :], start=True, stop=True), [R("zaug"), R("WAUG")], rp)
            ACT(lambda e: e.activation(out=A["e1"][:n, :], in_=psb(bp)[:n, :], func=AF.Exp, scale=-1.0), rp, [R("la")])
            ACT(lambda e: e.activation(out=A["la"][:n, :], in_=A["e1"][:n, :], func=AF.Ln, bias=1.0), [R("la")], [R("la")])
            return WGA, WGB, rwa, rwb, xr

        def gla_norm(A, po, rpo, c0, n, tb):
            st = A["st"]
            DVE(lambda e: e.memset(st[:, 0:8], 0.0), [], [R("st")])
            for h in range(4):
                ACT(lambda e, h=h: e.activation(out=A["junk"][:n, :], in_=po[:n, h * 256:(h + 1) * 256], func=AF.Square, accum_out=st[:n, h:h + 1]),
                    rpo + [R("st")], [R("junk"), R("st")])
            ACT(lambda e: e.activation(out=st[:n, 4:8], in_=st[:n, 0:4], func=AF.Ln, scale=1.0 / 256, bias=EPSD[:n, :]), [R("st"), R("EPSD")], [R("st")])
            ACT(lambda e: e.activation(out=st[:n, 4:8], in_=st[:n, 4:8], func=AF.Exp, scale=-0.5), [R("st")], [R("st")])
            for h in range(4):
                DVE(lambda e, h=h: e.scalar_tensor_tensor(out=A["on"][:n, h * 256:(h + 1) * 256], in0=po[:n, h * 256:(h + 1) * 256], scalar=st[:n, 4 + h:5 + h],
                                                          in1=GNB[:n, h * 256:(h + 1) * 256], op0=ALU.mult, op1=ALU.mult),
                    rpo + [R("st"), R("GNB")], [R("on")])

        def gla_back_T(A, c0, n, tb):
            bt, rt = pbank()
            ptv = psb(bt).bitcast(BF16)[:, 0:1024].rearrange("p (k n) -> p k n", k=KC)
            PE([lambda e, dc=dc: e.transpose(out=ptv[:, dc, :n], in_=A["on"][:n, dc * 128:(dc + 1) * 128], identity=IDB[:n, :n]) for dc in range(KC)],
               [R("on"), R("IDB")], rt)
            DVE(lambda e: e.tensor_tensor(out=BIG[:, 8:16, c0:c0 + n], in0=ptv[:, :, :n], in1=BIG[:, 8:16, c0:c0 + n], op=ALU.mult),
                rt + [RBIG[8 + k][tb] for k in range(KC)], [RBIG[8 + k][tb] for k in range(KC)])

        def gla_rproj(sbi):
            WGB = OUT[:].rearrange("p k n -> p (k n)").bitcast(BF16)[:, 0:KC * 2048].rearrange("p (k n) -> p k n", k=KC)
            for (c0, n, tb, is_s) in blocks_of(sbi):
                xr = [RXN[kc][tb] for kc in range(KC)]
                for dc in range(KC):
                    b, br = pbank()
                    mm8(psb(b)[:, :n], WGB, 1024 + dc * 128, 128, lambda kc: XN[:, kc, c0:c0 + n], [R("WGB")] + xr, br)
                    ACT(lambda e, b=b, dc=dc: e.activation(out=BIG[:, 8 + dc, c0:c0 + n], in_=psb(b)[:, :n], func=AF.Silu), br, [RBIG[8 + dc][tb]])

        def gla_zproj():
            WGA = BIG[:, 16:24, :]
            for tb in range(2):
                c0 = tb * 512
                zb = TMP[tb][:].bitcast(BF16)
                xr = [RXN[kc][tb] for kc in range(KC)]
                DVE(lambda e, zb=zb: e.memset(zb[0:32, 0:512], 1.0), [], [RTMP[tb]])
                bz, rz = pbank()
                mm8(psb(bz)[0:16, :], WGA, 1024, 16, lambda kc: XN[:, kc, c0:c0 + 512], [R("WGA")] + xr, rz)
                ACT(lambda e, zb=zb, bz=bz: e.activation(out=zb[0:16, 0:512], in_=psb(bz)[0:16, :], func=AF.Copy), rz, [RTMP[tb]])

        def gla_front_a(A, j):
            c0, n, tb = j * 128, 128, j // 4
            xr = [RXN[kc][tb] for kc in range(KC)]
            WGA = BIG[:, 16:24, :]
            WGB = OUT[:].rearrange("p k n -> p (k n)").bitcast(BF16)[:, 0:KC * 2048].rearrange("p (k n) -> p k n", k=KC)
            rwa = [R("WGA")]
            rwb = [R("WGB")]
            la = A["la"]
            zb = TMP[tb][:].bitcast(BF16)
            t0 = (j % 4) * 128
            bp, rp = pbank()
            PE(lambda e: e.matmul(psb(bp)[:n, :], lhsT=zb[0:17, t0:t0 + n], rhs=WAUG[0:17, :], start=True, stop=True), [RTMP[tb], R("WAUG")], rp)
            ACT(lambda e: e.activation(out=A["e1"][:n, :], in_=psb(bp)[:n, :], func=AF.Exp, scale=-1.0), rp, [R("la")])
            ACT(lambda e: e.activation(out=la[:n, :], in_=A["e1"][:n, :], func=AF.Ln, bias=1.0), [R("la")], [R("la")])

        def gla_front_b(A, j):
            c0, n, tb = j * 128, 128, j // 4
            xr = [RXN[kc][tb] for kc in range(KC)]
            WGA = BIG[:, 16:24, :]
            WGB = OUT[:].rearrange("p k n -> p (k n)").bitcast(BF16)[:, 0:KC * 2048].rearrange("p (k n) -> p k n", k=KC)
            rwa = [R("WGA")]
            rwb = [R("WGB")]
            la = A["la"]
            bvv, rvv = pbank(2)
            for half in range(2):
                PE([lambda e, kc=kc, half=half: e.matmul(psb(bvv + half), lhsT=XN[:, kc, c0:c0 + n], rhs=WGB[:, kc, half * 512:(half + 1) * 512],
                                                         start=(kc == 0), stop=(kc == KC - 1)) for kc in range(KC)], rwb + xr, [rvv[half]])
            DVE(lambda e: e.tensor_copy(out=A["vt"], in_=psb(bvv, 2)), rvv, [R("vt")])
            bb, rb = pbank()
            for h in range(4):
                PE(lambda e, h=h: e.matmul(psb(bb)[:, h * 128:(h + 1) * 128], lhsT=la[:, h * 128:(h + 1) * 128], rhs=TRII[:], start=True, stop=True),
                   [R("la"), R("TRII")], rb)
            bv, rv = pbank()
            PE(lambda e: e.matmul(psb(bv), lhsT=TRIR[:], rhs=la, start=True, stop=True), [R("la"), R("TRIR")], rv)
            bkt, rkt = pbank()
            PE([lambda e, kc=kc: e.matmul(psb(bkt), lhsT=XN[:, kc, c0:c0 + n], rhs=WGA[:, kc, 512:1024], start=(kc == 0), stop=(kc == KC - 1)) for kc in range(KC)],
               rwa + xr, rkt)
            ACT(lambda e: e.activation(out=A["eb"], in_=psb(bb), func=AF.Exp, bias=LNQ[:]), rb + [R("LNQ")], [R("eb")])
            ACT(lambda e: e.activation(out=A["enb"], in_=psb(bb), func=AF.Exp, scale=-1.0), rb, [R("enb")])
            pbl = psb(bb).rearrange("p (h t) -> p h t", h=4)[:, :, 127:128]
            ACT(lambda e: e.activation(out=A["ebl"].unsqueeze(2), in_=pbl, func=AF.Exp), rb, [R("ebl")])
            ACT(lambda e: e.activation(out=A["ek"], in_=psb(bv), func=AF.Exp), rv, [R("ek")])
            DVE(lambda e: e.tensor_tensor(out=A["kh"], in0=psb(bkt), in1=A["ek"], op=ALU.mult), rkt + [R("ek")], [R("kh")])
            bq, rq = pbank()
            for h in range(4):
                mm8(psb(bq)[:, h * 128:(h + 1) * 128], WGA, h * 128, 128, lambda kc: XN[:, kc, c0:c0 + n], rwa + xr, rq)
            DVE(lambda e: e.tensor_tensor(out=A["qt"], in0=psb(bq), in1=A["eb"], op=ALU.mult), rq + [R("eb")], [R("qt")])
            bk, rk = pbank()
            for h in range(4):
                mm8(psb(bk)[:, h * 128:(h + 1) * 128], WGA, 512 + h * 128, 128, lambda kc: XN[:, kc, c0:c0 + n], rwa + xr, rk)
            DVE(lambda e: e.tensor_tensor(out=A["kt"], in0=psb(bk), in1=A["enb"], op=ALU.mult), rk + [R("enb")], [R("kt")])

        def gla_mid(A, j):
            bs, rs_ = pbank()
            for h in range(4):
                PE(lambda e, h=h: e.matmul(psb(bs)[:, h * 128:(h + 1) * 128], lhsT=A["kt"][:, h * 128:(h + 1) * 128], rhs=A["qt"][:, h * 128:(h + 1) * 128],
                                           start=True, stop=True), [R("kt"), R("qt")], rs_)
            mb = MASKT[:].unsqueeze(1).to_broadcast([128, 4, 128])
            DVE(lambda e: e.tensor_tensor(out=A["sc"].rearrange("p (h t) -> p h t", h=4), in0=psb(bs).rearrange("p (h t) -> p h t", h=4), in1=mb, op=ALU.mult),
                rs_ + [R("MASKT")], [R("sc")])
            bd, rd = pbank(2)
            for h in range(4):
                PE(lambda e, h=h: e.matmul(psb(bd, 2)[:, h * 256:(h + 1) * 256], lhsT=A["kh"][:, h * 128:(h + 1) * 128], rhs=A["vt"][:, h * 256:(h + 1) * 256],
                                           start=True, stop=True), [R("kh"), R("vt")], [rd[h // 2]])
            bo, ro = pbank(2)
            for h in range(4):
                PE([lambda e, h=h: e.matmul(psb(bo, 2)[:, h * 256:(h + 1) * 256], lhsT=A["qt"][:, h * 128:(h + 1) * 128], rhs=SBF[:, h, :],
                                            start=True, stop=False),
                    lambda e, h=h: e.matmul(psb(bo, 2)[:, h * 256:(h + 1) * 256], lhsT=A["sc"][:, h * 128:(h + 1) * 128], rhs=A["vt"][:, h * 256:(h + 1) * 256],
                                            start=False, stop=True)],
                   [R("sc"), R("vt"), R("qt"), R("SBF")], [ro[h // 2]])
            for h in range(4):
                DVE(lambda e, h=h: e.scalar_tensor_tensor(out=SST[:, h, :], in0=SST[:, h, :], scalar=A["ebl"][:, h:h + 1], in1=psb(bd, 2)[:, h * 256:(h + 1) * 256],
                                                          op0=ALU.mult, op1=ALU.add), [R("SST"), R("ebl"), rd[h // 2]], [R("SST")])
            POOL(lambda e: e.tensor_copy(out=SBF[:], in_=SST[:]), [R("SST")], [R("SBF")])
            gla_norm(A, psb(bo, 2), ro, j * 128, 128, j // 4)

        def gla_sample(A):
            c0, n, tb = SBT, NS, 2
            WGA, WGB, rwa, rwb, xr = gla_common_front(A, c0, n, tb)
            la = A["la"]
            bb, rb = pbank()
            for h in range(4):
                PE(lambda e, h=h: e.matmul(psb(bb)[:, h * NS:(h + 1) * NS], lhsT=la[:NS, h * 128:(h + 1) * 128], rhs=I16N[:], start=True, stop=True),
                   [R("la"), R("I16N")], rb)
            ACT(lambda e: e.activation(out=A["ea"], in_=psb(bb)[:, 0:64], func=AF.Exp), rb, [R("ea")])
            bq, rq = pbank()
            for h in range(4):
                mm8(psb(bq)[:, h * NS:(h + 1) * NS], WGA, h * 128, 128, lambda kc: XN[:, kc, c0:c0 + n], rwa + xr, rq)
            ACT(lambda e: e.activation(out=A["qs"], in_=psb(bq)[:, 0:64], func=AF.Copy, scale=128.0 ** -0.5), rq, [R("qs")])
            bkt, rkt = pbank()
            PE([lambda e, kc=kc: e.matmul(psb(bkt)[:NS, :], lhsT=XN[:, kc, c0:c0 + n], rhs=WGA[:, kc, 512:1024], start=(kc == 0), stop=(kc == KC - 1)) for kc in range(KC)],
               rwa + xr, rkt)
            ACT(lambda e: e.activation(out=A["ktok"][:NS, :], in_=psb(bkt)[:NS, :], func=AF.Copy), rkt, [R("ktok")])
            bvv, rvv = pbank(2)
            for half in range(2):
                PE([lambda e, kc=kc, half=half: e.matmul(psb(bvv + half)[:NS, :], lhsT=XN[:, kc, c0:c0 + n], rhs=WGB[:, kc, half * 512:(half + 1) * 512],
                                                         start=(kc == 0), stop=(kc == KC - 1)) for kc in range(KC)], rwb + xr, [rvv[half]])
            ACT(lambda e: e.activation(out=A["vt"][:NS, :], in_=psb(bvv, 2)[:NS, :], func=AF.Copy), rvv, [R("vt")])
            bo, ro = pbank(2)
            pinned.add(bo)
            pinned.add(bo + 1)
            po = psb(bo, 2)
            qsv = A["qs"].rearrange("p (h t) -> p h t", h=4)
            qmvs = [A["qmj"].rearrange("p (h t) -> p h t", h=4), A["qmj2"].rearrange("p (h t) -> p h t", h=4)]
            rqm = [[R("qmj")], [R("kh")]]
            sjvs = [A["sjb"].rearrange("p (h v) -> p h v", h=4), A["sjb2"].rearrange("p (h v) -> p h v", h=4)]
            rsjb = [[R("sjb")], [R("qt"), R("kt")]]
            def load_state(j):
                sj = A["s0j"][j % 4]
                S.dma("sp", T_S0[j % 4], lambda e: e.dma_start(out=sj, in_=sgla_d[j].rearrange("h k v -> k h v")), reads=[], writes=A["s0j_res"][j % 4])
            load_state(0)
            load_state(1)
            load_state(2)
            eav = A["ea"].rearrange("p (h t) -> p h t", h=4)
            kmjs = [A["kmj"], A["junk"].bitcast(BF16)]
            rkm = [R("kmj"), R("junk")]
            dsb = {}

            def emit_ds(j):
                km = kmjs[j % 2]
                DVE(lambda e: e.tensor_scalar(out=km[:NS, :], in0=A["ktok"][:NS, :], scalar1=IDF[:NS, j:j + 1], scalar2=None, op0=ALU.mult),
                    [R("ktok"), R("IDF")], [rkm[j % 2]])
                bd, rd = pbank(2)
                for h in range(4):
                    PE(lambda e, h=h: e.matmul(psb(bd, 2)[:, h * 256:(h + 1) * 256], lhsT=km[:NS, h * 128:(h + 1) * 128], rhs=A["vt"][:NS, h * 256:(h + 1) * 256],
                                               start=True, stop=True), [rkm[j % 2], R("vt")], [rd[h // 2]])
                dsb[j] = (bd, rd)
            emit_ds(0)
            for j in range(NS):
                sj = A["s0j"][j % 4]
                rsjl = A["s0j_res"][j % 4]
                if j + 3 < NS:
                    load_state(j + 3)
                if j + 1 < NS:
                    emit_ds(j + 1)
                bd, rd = dsb.pop(j)
                for h in range(4):
                    DVE(lambda e, h=h: e.scalar_tensor_tensor(out=sj[:, h, :], in0=sj[:, h, :], scalar=eav[:, h, j:j + 1], in1=psb(bd, 2)[:, h * 256:(h + 1) * 256],
                                                              op0=ALU.mult, op1=ALU.add), rsjl + [R("ea"), rd[h // 2]], rsjl)
                S.dma("act", out_tok(), lambda e: e.dma_start(out=ogs_d[:, j * 4:(j + 1) * 4, :], in_=sj), reads=rsjl, writes=[R("ogs")])
                sjv = sjvs[j % 2]
                qmv = qmvs[j % 2]
                ACT(lambda e: e.activation(out=sjv, in_=sj, func=AF.Copy), rsjl, rsjb[j % 2])
                DVE(lambda e: e.tensor_tensor(out=qmv, in0=qsv, in1=ID16[:, j, :].unsqueeze(1).to_broadcast([128, 4, NS]), op=ALU.mult),
                    [R("qs"), R("ID16")], rqm[j % 2])
                for h in range(4):
                    PE(lambda e, h=h: e.matmul(po[:NS, h * 256:(h + 1) * 256], lhsT=qmv[:, h, :], rhs=sjv[:, h, :], start=(j == 0 and h % 2 == 0), stop=(j == NS - 1),
                                               skip_group_check=True), rqm[j % 2] + rsjb[j % 2], [ro[h // 2]])
            pinned.discard(bo)
            pinned.discard(bo + 1)
            gla_norm(A, po, ro, c0, n, tb)
            gla_back_T(A, c0, n, tb)

        def wga_load():
            WGA = BIG[:, 16:24, :]
            S.dma("pool", T_WGA, [lambda e: e.dma_start(out=WGA, in_=wga_d)], reads=[], writes=ALL_WGA + [R("WGA")])

        def mixer_steps(sbi):
            bl = blocks_of(sbi)
            WGA = BIG[:, 16:24, :]
            WGB = OUT[:].rearrange("p k n -> p (k n)").bitcast(BF16)[:, 0:KC * 2048].rearrange("p (k n) -> p k n", k=KC)

            def wgb_piece(p):
                WGB = OUT[:].rearrange("p k n -> p (k n)").bitcast(BF16)[:, 0:KC * 2048].rearrange("p (k n) -> p k n", k=KC)
                wr = (ALL_OUT + [R("WGB")]) if p == 0 else [R("WGB")]
                S.dma("pool", T_WGB, [lambda e: e.dma_start(out=WGB[:, :, p * 256:(p + 1) * 256], in_=wgb_d[:, :, p * 256:(p + 1) * 256])], reads=[], writes=wr)
            for c in range(KC):
                def compute(sv, rs, c=c):
                    wgb_piece(c)
                    wv = wview(sv, KC, 512)
                    wc = [VEC[:, V_WCONV + t * 8 + c:V_WCONV + t * 8 + c + 1] for t in range(3)]
                    ACT(lambda e: e.activation(out=UBUF[:, 0:2], in_=TAIL[:, c, :], func=AF.Copy), [R("TAIL")], [R("UBUF")])
                    for (c0, n, tb, is_s) in bl:
                        xr = [RXN[kc][tb] for kc in range(KC)]
                        bB, rB = pbank()
                        mm8(psb(bB)[:, :n], wv, 0, 128, lambda kc: XN[:, kc, c0:c0 + n], [rs] + xr, rB)
                        bC, rC = pbank()
                        mm8(psb(bC)[:, :n], wv, 128, 128, lambda kc: XN[:, kc, c0:c0 + n], [rs] + xr, rC)
                        bH, rH = pbank()
                        mm8(psb(bH)[:, :n], wv, 256, 128, lambda kc: XN[:, kc, c0:c0 + n], [rs] + xr, rH)
                        t0 = nxt("tmp", 3)
                        ACT(lambda e, bC=bC, t0=t0, n=n: e.activation(out=TMP[t0][:, :n], in_=psb(bC)[:, :n], func=AF.Copy), rC, [RTMP[t0]])
                        if not is_s:
                            DVE(lambda e, bH=bH, t0=t0, n=n, c0=c0: e.tensor_tensor(out=UBUF[:, 2 + c0:2 + c0 + n], in0=TMP[t0][:, :n], in1=psb(bH)[:, :n], op=ALU.mult),
                                [RTMP[t0]] + rH, [R("UBUF")])
                            t1 = nxt("tmp", 3)
                            ACT(lambda e, t1=t1, n=n, c0=c0: e.activation(out=TMP[t1][:, :n], in_=UBUF[:, c0:c0 + n], func=AF.Copy, scale=wc[0]), [R("UBUF"), R("VEC")], [RTMP[t1]])
                            DVE(lambda e, t1=t1, n=n, c0=c0: e.scalar_tensor_tensor(out=TMP[t1][:, :n], in0=UBUF[:, c0 + 1:c0 + 1 + n], scalar=wc[1], in1=TMP[t1][:, :n],
                                                                                    op0=ALU.mult, op1=ALU.add), [R("UBUF"), R("VEC"), RTMP[t1]], [RTMP[t1]])
                            DVE(lambda e, t1=t1, n=n, c0=c0: e.scalar_tensor_tensor(out=TMP[t1][:, :n], in0=UBUF[:, c0 + 2:c0 + 2 + n], scalar=wc[2], in1=TMP[t1][:, :n],
                                                                                    op0=ALU.mult, op1=ALU.add), [R("UBUF"), R("VEC"), RTMP[t1]], [RTMP[t1]])
                            DVE(lambda e, t1=t1, n=n, c0=c0, bB=bB: e.tensor_tensor(out=BIG[:, c, c0:c0 + n], in0=TMP[t1][:, :n], in1=psb(bB)[:, :n], op=ALU.mult),
                                [RTMP[t1]] + rB, [RBIG[c][tb]])
                        else:
                            DVE(lambda e, bH=bH, t0=t0: e.tensor_tensor(out=USM[:, c, 1, :], in0=TMP[t0][:, :NS], in1=psb(bH)[:, :NS], op=ALU.mult),
                                [RTMP[t0]] + rH, [R("USM")])
                            t1 = nxt("tmp", 3)
                            DVE(lambda e, t1=t1: e.tensor_scalar(out=TMP[t1][:, :NS], in0=SC[:, c, 0, :], scalar1=wc[0], scalar2=None, op0=ALU.mult), [R("SC"), R("VEC")], [RTMP[t1]])
                            DVE(lambda e, t1=t1: e.scalar_tensor_tensor(out=TMP[t1][:, :NS], in0=SC[:, c, 1, :], scalar=wc[1], in1=TMP[t1][:, :NS], op0=ALU.mult, op1=ALU.add),
                                [R("SC"), R("VEC"), RTMP[t1]], [RTMP[t1]])
                            DVE(lambda e, t1=t1: e.scalar_tensor_tensor(out=TMP[t1][:, :NS], in0=USM[:, c, 1, :], scalar=wc[2], in1=TMP[t1][:, :NS], op0=ALU.mult, op1=ALU.add),
                                [R("USM"), R("VEC"), RTMP[t1]], [RTMP[t1]])
                            DVE(lambda e, t1=t1, bB=bB: e.tensor_tensor(out=BIG[:, c, c0:c0 + NS], in0=TMP[t1][:, :NS], in1=psb(bB)[:, :NS], op=ALU.mult),
                                [RTMP[t1]] + rB, [RBIG[c][tb]])
                    ACT(lambda e: e.activation(out=TAIL[:, c, :], in_=UBUF[:, SBT:SBT + 2], func=AF.Copy), [R("UBUF")], [R("TAIL")])
                add_step([(lambda sv: wview(sv, KC, 512)[:, :, 0:384], slab(SL_CONV + c, KC, 512)[:, :, 0:384])], compute, "conv")
            def gla_all():
                S.barrier()
                A = arena_views()
                DVE(lambda e: e.memset(A["zaug"][0:32, :], 1.0), [], [R("zaug")])
                gla_rproj(sbi)
                gla_zproj()
                gla_front_a(A, 0)
                gla_front_b(A, 0)
                for j in range(8):
                    if j < 7:
                        gla_front_a(A, j + 1)
                    gla_mid(A, j)
                    if j < 7:
                        gla_front_b(A, j + 1)
                    gla_back_T(A, j * 128, 128, j // 4)
                if sbi == 0:
                    gla_sample(A)
                S.barrier()
            add_fence(0)
            add_work(gla_all, "gla")
            add_fence(1)
            for dc in range(KC):
                def compute(sv, rs, dc=dc):
                    wv = wview(sv, KC, 512)
                    for (c0, n, tb, is_s) in bl:
                        xr = [RXN[kc][tb] for kc in range(KC)]
                        b0, r0 = pbank()
                        mm8(psb(b0)[:, :n], wv, 0, 128, lambda kc: BIG[:, 8 + kc, c0:c0 + n], [rs] + [RBIG[8 + kc][tb] for kc in range(KC)], r0)
                        b1, r1 = pbank()
                        mm8(psb(b1)[:, :n], wv, 128, 128, lambda kc: BIG[:, kc, c0:c0 + n], [rs] + [RBIG[kc][tb] for kc in range(KC)], r1)
                        ba, ra = pbank()
                        mm8(psb(ba)[:, :n], wv, 256, 128, lambda kc: XN[:, kc, c0:c0 + n], [rs] + xr, ra)
                        bg, rg = pbank()
                        mm8(psb(bg)[:, :n], wv, 384, 128, lambda kc: XN[:, kc, c0:c0 + n], [rs] + xr, rg)
                        ta = nxt("tmp", 3)
                        tg = nxt("tmp", 3)
                        ACT(lambda e, ba=ba, ta=ta, n=n: e.activation(out=TMP[ta][:, :n], in_=psb(ba)[:, :n], func=AF.Sigmoid), ra, [RTMP[ta]])
                        ACT(lambda e, bg=bg, tg=tg, n=n: e.activation(out=TMP[tg][:, :n], in_=psb(bg)[:, :n], func=AF.Sigmoid), rg, [RTMP[tg]])
                        DVE(lambda e, b0=b0, ta=ta, n=n: e.tensor_tensor(out=TMP[ta][:, :n], in0=TMP[ta][:, :n], in1=psb(b0)[:, :n], op=ALU.mult), [RTMP[ta]] + r0, [RTMP[ta]])
                        DVE(lambda e, b1=b1, tg=tg, n=n: e.tensor_tensor(out=TMP[tg][:, :n], in0=TMP[tg][:, :n], in1=psb(b1)[:, :n], op=ALU.mult), [RTMP[tg]] + r1, [RTMP[tg]])
                        DVE(lambda e, ta=ta, tg=tg, n=n, c0=c0: e.tensor_tensor(out=BIG[:, 16 + dc, c0:c0 + n], in0=TMP[ta][:, :n], in1=TMP[tg][:, :n], op=ALU.add),
                            [RTMP[ta], RTMP[tg]], [RBIG[16 + dc][tb], R("WGA")])
                add_step([(lambda sv: wview(sv, KC, 512), slab(SL_MERGE + dc, KC, 512))], compute, "merge")
                if dc % 3 == 1:
                    pop_side()
            for q in range(2):
                def compute(sv, rs, q=q):
                    wv = wview(sv, KC, 512)
                    for dl in range(4):
                        dc = q * 4 + dl
                        for (c0, n, tb, is_s) in bl:
                            b, br = pbank()
                            mm8(psb(b)[:, :n], wv, dl * 128, 128, lambda kc: BIG[:, 16 + kc, c0:c0 + n], [rs] + [RBIG[16 + kc][tb] for kc in range(KC)], br)
                            out_epilogue(b, br, dc, (c0, n, tb, is_s))
                add_step([(lambda sv: wview(sv, KC, 512), slab(SL_OPROJ + q, KC, 512))], compute, "oproj")
            while side_q:
                pop_side()

        def swap_bufs():
            nonlocal X, OUT, RX, ROUT, ALL_OUT
            X, OUT = OUT, X
            RX, ROUT = ROUT, RX
            ALL_OUT = [r for row in ROUT for r in row]

        def next_sb_load(blk):
            swap_bufs()
            load_x_blk(1, blk)
            swap_bufs()

        def next_sb_prenorm(blk):
            swap_bufs()
            prenorm_blk(0, blk)
            swap_bufs()

        def load_x_blk(sbi, blk):
            (c0, n, tb, is_s) = blk
            if not is_s:
                for kc in range(KC):
                    S.dma("sp", ld_tok(), lambda e, kc=kc: e.dma_start(out=X[:, kc, c0:c0 + n], in_=xT_d[:, kc, sbi * SBT + c0:sbi * SBT + c0 + n]),
                          reads=[], writes=[RX[kc][tb]])
            else:
                S.dma("sp", ld_tok(), lambda e: e.dma_start(out=X[:, :, SBT:NT], in_=xT_d[:, :, SEQ:NCOL]), reads=[], writes=[RX[kc][2] for kc in range(KC)])

        def store_y_blk(sbi, blk):
            (c0, n, tb, is_s) = blk
            if not is_s:
                for kc in range(KC):
                    S.dma("sp", out_tok(), lambda e, kc=kc: e.dma_start(out=yT_d[:, kc, sbi * SBT + c0:sbi * SBT + c0 + n], in_=X[:, kc, c0:c0 + n]),
                          reads=[RX[kc][tb]], writes=[R("yT")])
            else:
                S.dma("sp", out_tok(), lambda e: e.dma_start(out=yT_d[:, :, SEQ:NCOL], in_=X[:, :, SBT:NT]), reads=[RX[kc][2] for kc in range(KC)], writes=[R("yT")])
                S.dma("sp", out_tok(), lambda e: e.dma_start(out=ocs_d, in_=USM[:]), reads=[R("USM")], writes=[R("ocs")])

        for q in range(4):
            ada_step(q)
        queue_ada([4, 5] + [6, 7, 8, 9] + [12, 13, 14, 15])
        for blk in blocks_of(0):
            add_work(lambda blk=blk: load_x_blk(0, blk))
            add_work(lambda blk=blk: prenorm_blk(0, blk), "prenorm_blk0")
        for sbi in range(2):
            ffn_steps(0, SL_F1A, SL_F1B, sbi, side_every=2 if sbi == 0 else 0, flush_side=True)
            if sbi == 0:
                queue_ada([10, 11])
            add_work(wga_load, "wga")
            for blk in blocks_of(sbi):
                add_work(lambda blk=blk: post_blk(0, blk), "post_blk0")
            for blk in blocks_of(sbi):
                add_work(lambda blk=blk: prenorm_blk(1, blk), "prenorm_blk1")
            mixer_steps(sbi)
            for blk in blocks_of(sbi):
                add_work(lambda blk=blk: post_blk(1, blk), "post_blk1")
            for blk in blocks_of(sbi):
                add_work(lambda blk=blk: prenorm_blk(2, blk), "prenorm_blk2")
            if sbi == 0:
                queue_ada([16, 17])
            ffn_steps(2, SL_F2A, SL_F2B, sbi, side_every=3 if sbi == 0 else 0, flush_side=True)
            for blk in blocks_of(sbi):
                add_work(lambda blk=blk: post_blk(2, blk), "post_blk2")
                if sbi == 0 and not blk[3]:
                    add_work(lambda blk=blk: next_sb_load(blk), "load_next")
                add_work(lambda blk=blk, sbi=sbi: store_y_blk(sbi, blk))
            if sbi == 0:
                for blk in blocks_of(1):
                    add_work(lambda blk=blk: next_sb_prenorm(blk), "prenorm_blk0")
                add_work(swap_bufs, "swap")
        run_steps()
        S.dma("sp", out_tok(), lambda e: e.dma_start(out=ogp_d, in_=SST[:]), reads=[R("SST")], writes=[R("ogp")])
        S.dma("sp", out_tok(), lambda e: e.dma_start(out=ocp_d, in_=TAIL[:]), reads=[R("TAIL")], writes=[R("ocp")])
        spE = S.engs["sp"]
        for t in T_OUT:
            S._wait(spE, t, t.count)
        S.barrier()

        with nc.Block() as block:
            def replay(en):
                def f(e):
                    for o in S.engs[en].ops:
                        if o[0] == "wait":
                            e.wait_ge(o[1].sem, o[2] * o[1].step)
                        else:
                            name, a, k = o[1]
                            ins = getattr(e, name)(*a, **k)
                            if o[2] is not None:
                                ins.then_inc(o[2].sem, o[2].step)
                return f
            block.tensor(replay("pe"))
            block.scalar(replay("act"))
            block.vector(replay("dve"))
            block.gpsimd(replay("pool"))
            block.sync(replay("sp"))
    build_nc.pe_labels = S.pe_labels
    return nc, list(dbg_d.keys())


_NC_CACHE = {}


def _consts():
    ident = np.eye(128, dtype=np.float32)
    s = np.arange(128)[:, None]
    t = np.arange(128)[None, :]
    maskT = (s <= t).astype(np.float32)
    triI = maskT * np.float32(-1.0 / 16.0)
    triR = (s > t).astype(np.float32) * np.float32(-1.0 / 16.0)
    i16n = np.eye(16, dtype=np.float32) * np.float32(-1.0 / 16.0)
    id16rep = np.ascontiguousarray(np.broadcast_to(np.eye(16, dtype=np.float32)[None], (128, 16, 16)))
    return dict(ident=ident, maskT=maskT, triI=triI.astype(np.float32), triR=triR.astype(np.float32), i16n=i16n, id16rep=id16rep)


def _slabs(w_ada, w1i, w1o, w2i, w2o, wmix, wbr, wmo):
    sl = np.zeros((NSL, 128, 4096), np.float32)

    def pkn(w, c0, n):
        K = w.shape[0]
        return w[:, c0:c0 + n].reshape(K // 128, 128, n).transpose(1, 0, 2)

    for q in range(18):
        sl[SL_ADA + q] = pkn(w_ada, q * 512, 512).reshape(128, -1)
    for (ba, bb, wi, wo) in ((SL_F1A, SL_F1B, w1i, w1o), (SL_F2A, SL_F2B, w2i, w2o)):
        for fp in range(11):
            t = np.concatenate([pkn(wi, fp * 256, 256), pkn(wi, DFF + fp * 256, 256)], axis=2)
            sl[ba + fp] = t.reshape(128, -1)
        for dc in range(8):
            sl[bb + dc, :, :FC * 128] = pkn(wo, dc * 128, 128).reshape(128, -1)
    z = np.zeros((128, KC, 128), np.float32)
    for c in range(8):
        t = np.concatenate([pkn(wmix, O_B + c * 128, 128), pkn(wmix, O_C + c * 128, 128), pkn(wmix, O_H + c * 128, 128), z], axis=2)
        sl[SL_CONV + c] = t.reshape(128, -1)
    for dc in range(8):
        t = np.concatenate([pkn(wbr[0], dc * 128, 128), pkn(wbr[1], dc * 128, 128), pkn(wmix, O_GA + dc * 128, 128), pkn(wmix, O_GB + dc * 128, 128)], axis=2)
        sl[SL_MERGE + dc] = t.reshape(128, -1)
    for q in range(2):
        sl[SL_OPROJ + q] = pkn(wmo, q * 512, 512).reshape(128, -1)
    wga = np.ascontiguousarray(np.concatenate([pkn(wmix, O_Q, 1024), pkn(wmix, O_Z, 16)], axis=2))
    wgb = np.ascontiguousarray(pkn(wmix, O_V, 2048))
    return sl, wga, wgb


def _fm(a):
    a = np.asarray(a, dtype=np.float32)
    lead = a.shape[:-1]
    a2 = a.reshape(-1, KC, 128)
    out = np.transpose(a2, (2, 1, 0))
    return np.ascontiguousarray(out.reshape((128, KC) + lead))


def kernel(x_prompt, x_sample, state_gla, state_conv, c_prompt, c_sample, w_ada, b_ada, g_pre, g_post,
           w_ffn1_in, w_ffn1_out, w_ffn2_in, w_ffn2_out, w_mix_in, w_alpha, b_alpha, g_gla_norm, w_conv,
           w_branch_out, w_mix_out, _dbg=()):
    f32 = lambda a: np.ascontiguousarray(np.asarray(a, dtype=np.float32))
    key = tuple(_dbg)
    if key not in _NC_CACHE:
        _NC_CACHE[key] = build_nc(_dbg)
    nc, dbg_names = _NC_CACHE[key]
    consts = _consts()
    x_prompt = f32(x_prompt); x_sample = f32(x_sample); state_gla = f32(state_gla); state_conv = f32(state_conv)
    c_prompt = f32(c_prompt); c_sample = f32(c_sample)
    vec = np.zeros((128, NVEC), np.float32)
    vec[:, V_BADA:V_BADA + 72] = f32(b_ada)[0].reshape(72, 128).T
    vec[:, V_GPRE:V_GPRE + 24] = f32(g_pre)[0].reshape(24, 128).T
    vec[:, V_GPOST:V_GPOST + 24] = f32(g_post)[0].reshape(24, 128).T
    vec[:, V_WCONV:V_WCONV + 24] = f32(w_conv)[0].reshape(24, 128).T
    gnb = np.ascontiguousarray(np.broadcast_to(f32(g_gla_norm)[0][None, :], (128, 1024)))
    walpha = np.ascontiguousarray(np.concatenate([f32(w_alpha)[0], f32(b_alpha)[0][None, :]], axis=0))
    wsl, wga, wgb = _slabs(f32(w_ada)[0], f32(w_ffn1_in)[0], f32(w_ffn1_out)[0], f32(w_ffn2_in)[0], f32(w_ffn2_out)[0],
                           f32(w_mix_in)[0], f32(w_branch_out)[0], f32(w_mix_out)[0])
    shared = dict(vecT=vec, gnb=gnb, walpha=walpha, wsl=wsl, wga=wga, wgb=wgb, **consts)
    in_maps = []
    for c in range(8):
        sl = slice(c * NS, (c + 1) * NS)
        xs = np.concatenate([x_prompt[c], x_sample[sl, 0, :]], axis=0)
        cc = np.concatenate([c_prompt[c:c + 1], c_sample[sl]], axis=0)
        m = dict(shared)
        m["xT"] = _fm(xs)
        m["cT"] = _fm(cc)
        m["sgla"] = np.ascontiguousarray(state_gla[0, sl])
        m["sconvT"] = np.ascontiguousarray(np.transpose(_fm(state_conv[0, sl]), (0, 1, 3, 2)))
        in_maps.append(m)
    res = run_bass_kernel_spmd(nc, in_maps, core_ids=list(range(8)))
    rs = res.results
    yp = np.zeros((8, SEQ, D), np.float32)
    ys = np.zeros((128, 1, D), np.float32)
    gp = np.zeros((1, 8, 4, 128, 256), np.float32)
    cp = np.zeros((1, 8, 2, D), np.float32)
    gs = np.zeros((1, 128, 4, 128, 256), np.float32)
    cs = np.zeros((1, 128, 2, D), np.float32)
    for c in range(8):
        r = rs[c]
        sl = slice(c * NS, (c + 1) * NS)
        yT = np.asarray(r["yT"])
        full = np.transpose(yT, (2, 1, 0)).reshape(NCOL, D)
        yp[c] = full[:SEQ]
        ys[sl, 0, :] = full[SEQ:]
        gp[0, c] = np.transpose(np.asarray(r["ogla_p"]), (1, 0, 2))
        cp[0, c] = np.transpose(np.asarray(r["oconv_p"]), (2, 1, 0)).reshape(2, D)
        gs[0, sl] = np.transpose(np.asarray(r["ogla_s"]).reshape(128, NS, 4, 256), (1, 2, 0, 3))
        cs[0, sl] = np.transpose(np.asarray(r["oconv_s"]), (3, 2, 1, 0)).reshape(NS, 2, D)
    if _dbg:
        kernel._dbg_out = [{n: np.asarray(rs[c]["dbg_" + n]) for n in dbg_names} for c in range(8)]
    return (yp, ys, gp, cp, gs, cs)
```
